# Optimizing a Trainium2 kernel written in Bass

```python
import math
import jax, jax.numpy as jnp
from jax import lax
import numpy as np

D_MODEL = 1024
BATCH = 2
SEQ = 8192
DEPTH = 2

RET_HEADS = 4
RET_DK = 64
RET_DV = 128
RET_CHUNK = 128
ATT_PATTERNS = ((128, 1), (512, 4), (2048, 16))
ATT_GROUPS = len(ATT_PATTERNS)
ATT_HEADS = 4
ATT_DH = 64
ATT_BLOCK = 128
POOL_WINDOWS = (2, 4, 8, 16)
POOL_CH = 64
D_FF = 2816
CONV_W = 3
ROPE_THETA = 10000.0
EPS = 1e-6
N_BRANCH = 3

RET_QK_W = RET_HEADS * RET_DK
RET_V_W = RET_HEADS * RET_DV
ATT_W = ATT_GROUPS * ATT_HEADS * ATT_DH
ATT_OUT_W = ATT_HEADS * ATT_DH
POOL_W = len(POOL_WINDOWS) * POOL_CH
IN_SPLITS = (RET_QK_W, RET_QK_W, RET_V_W, RET_V_W, ATT_W, ATT_W, ATT_W, POOL_W, N_BRANCH * D_MODEL)
D_IN = sum(IN_SPLITS)

kernel_name = "hybrid_retention_dilated_pool_block"


def rms_norm(x, g):
    xf = x.astype(jnp.float32)
    y = xf * lax.rsqrt(jnp.mean(xf * xf, axis=-1, keepdims=True) + EPS)
    return (y * g.astype(jnp.float32)).astype(x.dtype)


def rotary(x, positions):
    half = x.shape[-1] // 2
    inv = ROPE_THETA ** (-jnp.arange(half, dtype=jnp.float32) / half)
    ang = positions.astype(jnp.float32)[:, :, None] * inv
    cos = jnp.cos(ang)[:, :, None, :].astype(x.dtype)
    sin = jnp.sin(ang)[:, :, None, :].astype(x.dtype)
    x1, x2 = x[..., :half], x[..., half:]
    return jnp.concatenate([x1 * cos - x2 * sin, x2 * cos + x1 * sin], axis=-1)


def retention(q, k, v, g):
    B, S, H, dk = q.shape
    dv = v.shape[-1]
    C = RET_CHUNK
    n = S // C
    log_gamma = jnp.log(1.0 - 2.0 ** (-5.0 - jnp.arange(H, dtype=jnp.float32)))
    qf = q.astype(jnp.float32).reshape(B, n, C, H, dk)
    kf = (k.astype(jnp.float32) * dk ** -0.5).reshape(B, n, C, H, dk)
    vf = v.astype(jnp.float32).reshape(B, n, C, H, dv)
    idx = jnp.arange(C, dtype=jnp.float32)
    rel = idx[:, None] - idx[None, :]
    decay = jnp.where(rel >= 0, jnp.exp(log_gamma[:, None, None] * jnp.maximum(rel, 0.0)), 0.0)
    scores = jnp.einsum('bnihd,bnjhd->bnhij', qf, kf) * decay
    y_inner = jnp.einsum('bnhij,bnjhe->bnihe', scores, vf)
    k_dec = jnp.exp(log_gamma[None, :] * (C - 1.0 - idx)[:, None])
    kv = jnp.einsum('bnjhd,jh,bnjhe->bnhde', kf, k_dec, vf)
    chunk_decay = jnp.exp(log_gamma * C)[None, :, None, None]

    def step(state, kv_n):
        return state * chunk_decay + kv_n, state

    _, prev = lax.scan(step, jnp.zeros((B, H, dk, dv), jnp.float32), jnp.moveaxis(kv, 1, 0))
    q_dec = jnp.exp(log_gamma[None, :] * (idx + 1.0)[:, None])
    y_cross = jnp.einsum('bnihd,ih,nbhde->bnihe', qf, q_dec, prev)
    y = (y_inner + y_cross).reshape(B, S, H, dv)
    mu = jnp.mean(y, axis=-1, keepdims=True)
    var = jnp.mean(jnp.square(y - mu), axis=-1, keepdims=True)
    y = ((y - mu) * lax.rsqrt(var + EPS)).reshape(B, S, H * dv)
    return (jax.nn.silu(g.astype(jnp.float32)) * y).astype(v.dtype)


def banded_attention(q, k, v, win):
    N, L, H, dh = q.shape
    BLK = ATT_BLOCK
    nb = -(-L // BLK)
    Lp = nb * BLK
    pad = ((0, 0), (0, Lp - L), (0, 0), (0, 0))
    qb = jnp.pad(q, pad).reshape(N, nb, BLK, H, dh)

    def band(t):
        tb = jnp.pad(t, pad).reshape(N, nb, BLK, H, dh)
        prev = jnp.pad(tb[:, :-1], ((0, 0), (1, 0), (0, 0), (0, 0), (0, 0)))
        return jnp.concatenate([prev, tb], axis=2)

    kk, vv = band(k), band(v)
    s = jnp.einsum('nbqhd,nbkhd->nbhqk', qb, kk).astype(jnp.float32) * (dh ** -0.5)
    blk = jnp.arange(nb)[:, None, None]
    qpos = blk * BLK + jnp.arange(BLK)[None, :, None]
    kpos = (blk - 1) * BLK + jnp.arange(2 * BLK)[None, None, :]
    rel = qpos - kpos
    mask = (rel >= 0) & (rel <= win) & (kpos >= 0)
    s = jnp.where(mask[None, :, None], s, -jnp.inf)
    m = jnp.max(s, axis=-1, keepdims=True)
    p = jnp.exp(s - m)
    den = jnp.sum(p, axis=-1, keepdims=True)
    o = jnp.einsum('nbhqk,nbkhd->nbqhd', p / den, vv.astype(jnp.float32))
    lse = (m + jnp.log(den))[..., 0]
    o = o.reshape(N, Lp, H, dh)[:, :L]
    lse = jnp.transpose(lse, (0, 1, 3, 2)).reshape(N, Lp, H)[:, :L]
    return o, lse


def dilated_attention(q, k, v):
    B, S, G, H, dh = q.shape
    outs, lses = [], []
    for gi, (window, dil) in enumerate(ATT_PATTERNS):
        L = S // dil

        def to_sub(t):
            return jnp.transpose(t[:, :, gi].reshape(B, L, dil, H, dh), (0, 2, 1, 3, 4)).reshape(B * dil, L, H, dh)

        o, lse = banded_attention(to_sub(q), to_sub(k), to_sub(v), window // dil)
        outs.append(jnp.transpose(o.reshape(B, dil, L, H, dh), (0, 2, 1, 3, 4)).reshape(B, S, H, dh))
        lses.append(jnp.transpose(lse.reshape(B, dil, L, H), (0, 2, 1, 3)).reshape(B, S, H))
    w = jax.nn.softmax(jnp.stack(lses, axis=0), axis=0)
    y = jnp.sum(w[..., None] * jnp.stack(outs, axis=0), axis=0)
    return y.reshape(B, S, H * dh).astype(q.dtype)


def pool_mixer(u, lin, scale):
    B, S, _ = u.shape
    uf = u.astype(jnp.float32)
    c = jnp.pad(jnp.cumsum(uf, axis=1), ((0, 0), (1, 0), (0, 0)))
    t = jnp.arange(S)
    outs = []
    for gi, w in enumerate(POOL_WINDOWS):
        sl = slice(gi * POOL_CH, (gi + 1) * POOL_CH)
        start = jnp.maximum(t + 1 - w, 0)
        window_sum = c[:, 1:, sl] - c[:, start, sl]
        cnt = (t + 1 - start).astype(jnp.float32)[None, :, None]
        outs.append(window_sum / cnt - uf[:, :, sl])
    p = jnp.stack(outs, axis=2)
    y = jnp.einsum('bsgc,gce->bsge', p, lin.astype(jnp.float32)).reshape(B, S, POOL_W)
    return (y * scale.astype(jnp.float32)).astype(u.dtype)


def conv_glu_ffn(h, w_up, conv_w, conv_b, w_down):
    S = h.shape[1]
    u = h @ w_up
    up = jnp.pad(u, ((0, 0), (CONV_W - 1, 0), (0, 0)))
    c = conv_b + sum(conv_w[j] * up[:, j:j + S] for j in range(CONV_W))
    a, b = jnp.split(c, 2, axis=-1)
    return (jax.nn.silu(a) * b) @ w_down


def hybrid_layer(x, positions, norm1_g, w_in, b_gate, p_ret, p_att, p_pool, pool_lin, pool_scale,
                 w_o, norm2_g, w_up, conv_w, conv_b, w_down):
    B, S, D = x.shape
    hn = rms_norm(x, norm1_g)
    z = hn @ w_in
    rq, rk, rv, rg, aq, ak, av, pu, gates = jnp.split(z, np.cumsum(IN_SPLITS)[:-1].tolist(), axis=-1)
    rq = rotary(rq.reshape(B, S, RET_HEADS, RET_DK), positions)
    rk = rotary(rk.reshape(B, S, RET_HEADS, RET_DK), positions)
    y_ret = retention(rq, rk, rv.reshape(B, S, RET_HEADS, RET_DV), rg)
    aq = rotary(aq.reshape(B, S, ATT_GROUPS * ATT_HEADS, ATT_DH), positions).reshape(B, S, ATT_GROUPS, ATT_HEADS, ATT_DH)
    ak = rotary(ak.reshape(B, S, ATT_GROUPS * ATT_HEADS, ATT_DH), positions).reshape(B, S, ATT_GROUPS, ATT_HEADS, ATT_DH)
    av = av.reshape(B, S, ATT_GROUPS, ATT_HEADS, ATT_DH)
    y_att = dilated_attention(aq, ak, av)
    y_pool = pool_mixer(pu, pool_lin, pool_scale)
    g = jax.nn.sigmoid(gates + b_gate).reshape(B, S, N_BRANCH, D)
    m = g[:, :, 0] * (y_ret @ p_ret) + g[:, :, 1] * (y_att @ p_att) + g[:, :, 2] * (y_pool @ p_pool)
    x = x + m @ w_o
    x = x + conv_glu_ffn(rms_norm(x, norm2_g), w_up, conv_w, conv_b, w_down)
    return x


def setup_inputs(seed: int = 0) -> dict:
    key = jax.random.key(seed)
    ks = jax.random.split(key, 20)
    f32 = jnp.float32

    def nrm(k, shape, fan_in):
        return jax.random.normal(k, shape, f32) * (fan_in ** -0.5)

    x = jax.random.normal(ks[0], (BATCH, SEQ, D_MODEL), f32)
    start = jax.random.randint(ks[1], (BATCH,), 0, 1024, dtype=jnp.int32)
    positions = (start[:, None] + jnp.arange(SEQ, dtype=jnp.int32)[None, :]).astype(jnp.int32)
    return {
        "x": x,
        "positions": positions,
        "norm1_g": 1.0 + 0.02 * jax.random.normal(ks[2], (DEPTH, D_MODEL), f32),
        "w_in": nrm(ks[3], (DEPTH, D_MODEL, D_IN), D_MODEL),
        "b_gate": 0.1 * jax.random.normal(ks[4], (DEPTH, N_BRANCH * D_MODEL), f32),
        "p_ret": nrm(ks[5], (DEPTH, RET_V_W, D_MODEL), RET_V_W),
        "p_att": nrm(ks[6], (DEPTH, ATT_OUT_W, D_MODEL), ATT_OUT_W),
        "p_pool": nrm(ks[7], (DEPTH, POOL_W, D_MODEL), POOL_W),
        "pool_lin": nrm(ks[8], (DEPTH, len(POOL_WINDOWS), POOL_CH, POOL_CH), POOL_CH),
        "pool_scale": 1.0 + 0.1 * jax.random.normal(ks[9], (DEPTH, POOL_W), f32),
        "w_o": nrm(ks[10], (DEPTH, D_MODEL, D_MODEL), D_MODEL),
        "norm2_g": 1.0 + 0.02 * jax.random.normal(ks[11], (DEPTH, D_MODEL), f32),
        "w_up": nrm(ks[12], (DEPTH, D_MODEL, 2 * D_FF), D_MODEL),
        "conv_w": nrm(ks[13], (DEPTH, CONV_W, 2 * D_FF), CONV_W),
        "conv_b": 0.02 * jax.random.normal(ks[14], (DEPTH, 2 * D_FF), f32),
        "w_down": nrm(ks[15], (DEPTH, D_FF, D_MODEL), D_FF),
        "final_norm_g": 1.0 + 0.02 * jax.random.normal(ks[16], (D_MODEL,), f32),
    }


def reference(x, positions, norm1_g, w_in, b_gate, p_ret, p_att, p_pool, pool_lin, pool_scale,
              w_o, norm2_g, w_up, conv_w, conv_b, w_down, final_norm_g):
    for l in range(DEPTH):
        x = hybrid_layer(x, positions, norm1_g[l], w_in[l], b_gate[l], p_ret[l], p_att[l], p_pool[l],
                         pool_lin[l], pool_scale[l], w_o[l], norm2_g[l], w_up[l], conv_w[l], conv_b[l], w_down[l])
    return rms_norm(x, final_norm_g)
```

```python
import math
import numpy as np
import ml_dtypes
import concourse.bass as bass
import concourse.mybir as mybir
from concourse.bass_utils import run_bass_kernel_spmd

F32 = mybir.dt.float32
BF = mybir.dt.bfloat16
I32 = mybir.dt.int32
ALU = mybir.AluOpType
AF = mybir.ActivationFunctionType
AX = mybir.AxisListType

P = 128
T = 2048
NT = 16
D = 1024
DIN = 7168
DFF = 2816
DEPTH = 2
EPS = 1e-6
NCORES = 8
GD = (1, 4, 16)
KVW = 516
ENG = ['pe', 'act', 'dve', 'pool', 'sp']
import os
STOP = os.environ.get('KSTOP', '')


class _Stop(Exception):
    pass


def mark(tag):
    if STOP == tag:
        raise _Stop()

R = 6


class Sched:
    def __init__(self):
        self.ops = {e: [] for e in ENG}
        self.lastw = {}
        self.readers = {}
        self.dmas = {e: [] for e in ENG}
        self.ncc = 0

    def op(self, eng, fn, r=(), w=(), kind='c'):
        deps = set()
        for b in r:
            if b in self.lastw:
                deps.add(self.lastw[b])
        for b in w:
            if b in self.lastw:
                deps.add(self.lastw[b])
            for x in self.readers.get(b, ()):
                deps.add(x)
        me = (eng, len(self.ops[eng]))
        if eng == 'pe':
            deps = {d for d in deps if d[0] != 'pe'}
        rec = dict(fn=fn, deps=deps, kind=kind, sig=False)
        if kind == 'd':
            j = len(self.dmas[eng])
            rec['dj'] = j
            if j >= R:
                deps.add(self.dmas[eng][j - R])
            self.dmas[eng].append(me)
            rec['sig'] = True
        elif kind == 'cc':
            rec['cj'] = self.ncc
            self.ncc += 1
            rec['sig'] = True
        deps.discard(me)
        for d in deps:
            self.ops[d[0]][d[1]]['sig'] = True
        self.ops[eng].append(rec)
        for b in r:
            self.readers.setdefault(b, []).append(me)
        for b in w:
            self.lastw[b] = me
            self.readers[b] = []
        return me

    def finalize(self):
        for e in ENG:
            c = 0
            for rec in self.ops[e]:
                if rec['kind'] == 'c':
                    if rec['sig']:
                        c += 1
                        rec['sv'] = (('c', e), c, 1)
                elif rec['kind'] == 'd':
                    j = rec['dj']
                    rec['sv'] = (('d', e, j % R), 16 * (j // R + 1), 16)
                else:
                    rec['sv'] = (('cc', rec['cj']), 1, 1)

    def emit(self, e, eng, sems):
        waited = {}
        for rec in self.ops[e]:
            need = {}
            for d in rec['deps']:
                sid, val, _ = self.ops[d[0]][d[1]]['sv']
                need[sid] = max(need.get(sid, 0), val)
            for sid, val in need.items():
                if waited.get(sid, 0) < val:
                    eng.wait_ge(sems[sid], val)
                    waited[sid] = val
            inst = rec['fn'](eng)
            if rec['sig']:
                sid, val, inc = rec['sv']
                inst.then_inc(sems[sid], inc)


def build_program():
    nc = bass.Bass("TRN2", target_bir_lowering=False)
    S = Sched()

    def din(name, shape, dt=F32):
        return nc.dram_tensor(name, list(shape), dt, kind="ExternalInput").ap()

    x_d = din("x", [T, D])
    pos_d = din("pos", [P, NT], I32)
    w_in_d = din("w_in", [DEPTH, D, DIN])
    p_ret_d = din("p_ret", [DEPTH, 512, D])
    p_att_d = din("p_att", [DEPTH, 256, D])
    p_pool_d = din("p_pool", [DEPTH, 256, D])
    w_o_d = din("w_o", [DEPTH, D, D])
    w_up_d = din("w_up", [DEPTH, D, 2 * DFF])
    w_down_d = din("w_down", [DEPTH, DFF, D])
    g1_d = din("g1b", [DEPTH, P, D])
    g2_d = din("g2b", [DEPTH, P, D])
    gf_d = din("gfb", [P, D])
    bgate_d = din("bgate", [DEPTH, P, 24])
    pscale_d = din("pscale", [DEPTH, P, 2])
    convw_d = din("convw", [DEPTH, P, 3 * 44])
    convb_d = din("convb", [DEPTH, P, 44])
    plblk_d = din("plblk", [DEPTH, P, 256], BF)
    ident_d = din("ident", [P, P], BF)
    mask2_d = din("mask2", [P, 256], BF)
    invf_d = din("invf", [P, 32])
    dec8_d = din("dec8", [P, 8])
    gtab_d = din("gtab", [64, 512])
    bmat_d = din("bmat", [P, 8 * P], BF)
    invc_d = din("invc", [P, 8])
    sel_d = din("sel", [P, 3])
    coef_d = din("coef", [64, 12])
    out_d = nc.dram_tensor("out", [T, D], F32, kind="ExternalOutput").ap()

    Qd, KVd, HB, HBall, Od, SX, SXall, CX, CXall, RVd, hTd = [], [], [], [], [], [], [], [], [], [], []
    for l in range(DEPTH):
        Qd.append(nc.dram_tensor(f"Qd{l}", [T, 768], BF).ap())
        KVd.append(nc.dram_tensor(f"KVd{l}", [T, 3 * KVW], BF).ap())
        HB.append(nc.dram_tensor(f"HB{l}", [2688, KVW], BF).ap())
        HBall.append([nc.dram_tensor(f"HBall{l}_{c}", [4 * (512 if c < 5 else 128), KVW], BF).ap() for c in range(6)])
        Od.append(nc.dram_tensor(f"Od{l}", [T, 780], F32).ap())
        SX.append(nc.dram_tensor(f"SX{l}", [P, 768], F32).ap())
        SXall.append(nc.dram_tensor(f"SXall{l}", [4 * P, 768], F32).ap())
        CX.append(nc.dram_tensor(f"CX{l}", [P, 16], BF).ap())
        CXall.append(nc.dram_tensor(f"CXall{l}", [4 * P, 16], BF).ap())
        RVd.append(nc.dram_tensor(f"RVd{l}", [T, 1024], BF).ap())
        hTd.append(nc.dram_tensor(f"hTd{l}", [P, 8 * T], BF).ap().rearrange("p (k n) -> p k n", k=8))

    cur = [16512]

    def sb(name, shape, dt, at=None):
        nbytes = int(np.prod(shape[1:])) * (4 if dt in (F32, I32) else 2)
        nbytes = (nbytes + 31) // 32 * 32
        if at is None:
            off = cur[0]
            cur[0] += nbytes
        else:
            off = at
        assert off + nbytes <= 229344, (name, off, nbytes)
        return nc.alloc_sbuf_tensor_at(name, list(shape), dt, offset=off), off + nbytes

    def sbt(name, shape, dt):
        return sb(name, shape, dt)[0]

    x_sb = sbt("x_sb", [P, NT, D], F32)
    ident = sbt("ident", [P, P], BF)
    mask2 = sbt("mask2", [P, 256], BF)
    selI = sbt("selI", [P, 3, P], BF)
    cos_t = sbt("cos_t", [P, NT, 32], F32)
    sin_t = sbt("sin_t", [P, NT, 32], F32)
    ang_off = cur[0]
    ang_t = sbt("ang_t", [P, NT, 32], F32)
    invf = sbt("invf", [P, 32], F32)
    pos_i = sbt("pos_i", [P, NT], I32)
    pos_f = sbt("pos_f", [P, NT], F32)
    gb1 = sbt("gb1", [P, D], F32)
    gb2 = sbt("gb2", [P, D], F32)
    bgate = sbt("bgate", [P, 24], F32)
    pscale = sbt("pscale", [P, 2], F32)
    convw = sbt("convw", [P, 132], F32)
    convb = sbt("convb", [P, 44], F32)
    plblk = sbt("plblk", [P, 2, P], BF)
    bmat = sbt("bmat", [P, 8, P], BF)
    invc = sbt("invc", [P, 8], F32)
    dec8 = sbt("dec8", [P, 8], F32)
    sel = sbt("sel", [P, 3], F32)
    coef = sbt("coef", [64, 12], F32)
    gtab = sbt("gtab", [64, 512], F32)
    ss = sbt("ss", [P, 1], F32)
    rs = sbt("rs", [P, 1], F32)
    ss1 = sbt("ss1", [P, 1], F32)
    rs1 = sbt("rs1", [P, 1], F32)
    junk = nc.alloc_sbuf_tensor_at("junk", [P, D], BF, offset=ang_off)
    h_bf = sbt("h_bf", [P, D], BF)
    Sst = sbt("Sst", [64, 512], F32)
    Sbf = sbt("Sbf", [64, 512], BF)
    halo_p = sbt("halo_p", [P, 16], BF)
    cxs = sbt("cxs", [P, 16], BF)
    cxc = [sbt(f"cxc{i}", [P, 16], BF) for i in range(3)]
    cxf = sbt("cxf", [P, 16], F32)
    baseAB = cur[0]
    qT = sbt("qT", [64, 4, T], BF)
    kT = sbt("kT", [64, 4, T], BF)
    wb = [sbt(f"wb{i}", [P, 8, 512], BF) for i in range(2)]
    base2 = cur[0]
    hT_all = sbt("hT_all", [P, 8, T], BF)
    ktok = sbt("ktok", [P, NT, 256], BF)
    rvt = [sbt(f"rvt{i}", [P, 1024], BF) for i in range(4)]
    vst = [sbt(f"vst{i}", [P, 2, 4, 65], BF) for i in range(2)]
    puf = sbt("puf", [P, 256], F32)
    rA = sbt("rA", [P, 512], F32)
    rB = sbt("rB", [P, 512], F32)
    rC = sbt("rC", [P, 512], F32)
    obf = [sbt(f"obf{i}", [P, 512], BF) for i in range(4)]
    zf = [sbt(f"zf{i}", [P, 512], F32) for i in range(2)]
    zt = sbt("zt", [P, 512], F32)
    endA = cur[0]
    cur[0] = base2
    qblk = [sbt(f"qblk{i}", [P, 256], BF) for i in range(3)]
    kvo = [sbt(f"kvo{i}", [P, KVW], BF) for i in range(6)]
    kvc = [[sbt(f"kvc{j}_{i}", [P, KVW], BF) for i in range(3)] for j in range(2)]
    vprev = [sbt(f"vprev{i}", [P, 260], BF) for i in range(3)]
    QT = [sbt(f"QT{i}", [64, 4, P], BF) for i in range(2)]
    KTo = [sbt(f"KTo{i}", [64, 4, P], BF) for i in range(6)]
    KTh = [sbt(f"KTh{i}", [64, 4, P], BF) for i in range(2)]
    Pe = [sbt(f"Pe{i}", [P, 1024], BF) for i in range(2)]
    Pm = [sbt(f"Pm{i}", [P, 1024], BF) for i in range(2)]
    osb = [sbt(f"osb{i}", [P, 260], F32) for i in range(2)]
    endT = cur[0]
    cur[0] = base2
    puh = sbt("puh", [P, 256], BF)
    hTg = sbt("hTg", [P, 8, 512], BF)
    yretT = sbt("yretT", [P, 4, 512], BF)
    yattT = sbt("yattT", [P, 2, 512], BF)
    ypoolT = sbt("ypoolT", [P, 2, 512], BF)
    mT_off = cur[0]
    mT = sbt("mT", [P, 8, 512], BF)
    sxc = nc.alloc_sbuf_tensor_at("sxc", [P, 768], F32, offset=mT_off)
    rvk = [sbt(f"rvk{i}", [P, 768], BF) for i in range(2)]
    pus = [sbt(f"pus{i}", [P, 256], BF) for i in range(4)]
    sg = [sbt(f"sg{i}", [P, 512], F32) for i in range(2)]
    PT = [sbt(f"PT{i}", [P, 512], BF) for i in range(2)]
    ycp = [sbt(f"ycp{i}", [P, 512], F32) for i in range(2)]
    ysq = [sbt(f"ysq{i}", [P, 512], F32) for i in range(2)]
    st4 = [sbt(f"st4{i}", [P, 24], F32) for i in range(2)]
    yrb = [sbt(f"yrb{i}", [P, 512], BF) for i in range(2)]
    Ot = [sbt(f"Ot{i}", [P, 780], F32) for i in range(2)]
    num = [sbt(f"num{i}", [P, 260], F32) for i in range(2)]
    rden = [sbt(f"rden{i}", [P, 4], F32) for i in range(2)]
    yab = [sbt(f"yab{i}", [P, 256], BF) for i in range(2)]
    ppb = [sbt(f"ppb{i}", [P, 256], BF) for i in range(2)]
    ppT = [sbt(f"ppT{i}", [P, 2, P], BF) for i in range(2)]
    wg = sbt("wg", [P, 8, 3, P], BF)
    wp = sbt("wp", [P, 8, P], BF)
    sigt = sbt("sigt", [P, 3, 512], BF)
    sig = [sigt[:, i, :] for i in range(3)]
    signm = [("sig", i) for i in range(3)]
    m0, m1 = ycp[0], ysq[0]
    endB = cur[0]
    cur[0] = baseAB
    wd = sbt("wd", [P, 22, D], BF)
    h2T = sbt("h2T", [P, 8, 514], BF)
    gT = sbt("gT", [P, 22, 512], BF)
    wua = [sbt(f"wua{i}", [P, 8, 256], BF) for i in range(2)]
    wub = [sbt(f"wub{i}", [P, 8, 256], BF) for i in range(2)]
    uba = sbt("uba", [P, 514], F32)
    ubb = sbt("ubb", [P, 514], F32)
    ca = sbt("ca", [P, 512], F32)
    cb = sbt("cb", [P, 512], F32)
    sa = sbt("sa", [P, 512], F32)
    yout = [sbt(f"yout{i}", [P, D], F32) for i in range(2)]
    endC = cur[0]
    print("SBUF map: baseAB", baseAB, "base2", base2, "endA", endA, "endT", endT, "endB", endB, "endC", endC)

    ps = [nc.alloc_psum_tensor(f"ps{i}", [P, 512], F32) for i in range(4)]
    pS2 = nc.alloc_psum_tensor("pS2", [P, 1024], F32)
    pt = [nc.alloc_psum_tensor(f"pt{i}", [P, 1024], BF) for i in range(2)]

    cnt = {'ps': 0, 'pt': 0, 'wb': 0, 'ob': 0, 'zf': 0}

    def nxt(k, n):
        v = cnt[k] % n
        cnt[k] += 1
        return v

    op = S.op

    allbufs = set()

    def ALLBUF():
        return list(allbufs)

    def reg(*names):
        for n in names:
            allbufs.add(n)

    def dma(eng, out, in_, r, w):
        reg(*r)
        reg(*w)
        return op(eng, lambda e_: e_.dma_start(out=out, in_=in_), r=r, w=w, kind='d')

    def cop(eng, fn, r, w):
        reg(*r)
        reg(*w)
        return op(eng, fn, r=r, w=w)

    def load_const(dst, src, name):
        dma('sp', dst, src, [], [name])

    def fullbar(skip=()):
        def bufs():
            return [n for n in ALLBUF() if not (isinstance(n, tuple) and n and n[0] in skip)]
        cop('act', lambda e_: e_.activation(out=ss[:, 0:1], in_=ident[:, 0:1], func=AF.Copy), bufs(), bufs())
        cop('dve', lambda e_: e_.memset(ss[:, :], 0.0), bufs(), bufs())
        cop('pool', lambda e_: e_.memset(rs[:, :], 0.0), bufs(), bufs())
        cop('pe', lambda e_: e_.transpose(out=pt[0][:, 0:P], in_=ident[:, :], identity=ident[:, :]), bufs(), bufs())
        dma('sp', ss[:, :], ss[:, :], bufs(), bufs())

    load_const(ident[:, :], ident_d, "ident")
    load_const(mask2[:, :], mask2_d, "mask2")
    load_const(invf[:, :], invf_d, "invf")
    load_const(pos_i[:, :], pos_d, "pos_i")
    load_const(bmat[:, :, :], bmat_d.rearrange("p (g n) -> p g n", g=8), "bmat")
    load_const(invc[:, :], invc_d, "invc")
    load_const(dec8[:, :], dec8_d, "dec8")
    load_const(sel[:, :], sel_d, "sel")
    load_const(coef[:, :], coef_d, "coef")
    load_const(gtab[:, :], gtab_d, "gtab")
    for t in range(NT):
        dma('sp', x_sb[:, t, :], x_d[t * P:(t + 1) * P, :], [], [("x", t)])
    for s in range(3):
        cop('dve', lambda e_, s=s: e_.tensor_scalar(out=selI[:, s, :], in0=ident[:, :], scalar1=sel[:, s:s + 1],
                                                     scalar2=None, op0=ALU.mult),
            ["ident", "sel"], [("selI", s)])
    cop('dve', lambda e_: e_.tensor_copy(out=pos_f[:, :], in_=pos_i[:, :]), ["pos_i"], ["pos_f"])
    for t in range(NT):
        cop('dve', lambda e_, t=t: e_.tensor_scalar(out=ang_t[:, t, :], in0=invf[:, :], scalar1=pos_f[:, t:t + 1],
                                                     scalar2=None, op0=ALU.mult),
            ["invf", "pos_f"], [("ang", t)])
    angs = [("ang", t) for t in range(NT)]
    C1 = 6.28125
    C2 = 2 * math.pi - C1
    rA2 = rA[:, :].rearrange("p (t f) -> p t f", t=NT)
    rBi = rB[:, :].bitcast(I32).rearrange("p (t f) -> p t f", t=NT)
    for dst, shift, nm in ((sin_t, 0.0, "sin"), (cos_t, 0.5 * math.pi, "cos")):
        cop('dve', lambda e_, dst=dst, shift=shift: e_.tensor_scalar(out=dst[:, :, :], in0=ang_t[:, :, :], scalar1=shift, scalar2=None,
                                                                     op0=ALU.add), angs, [nm + "_a"])
        cop('dve', lambda e_, dst=dst: e_.tensor_scalar(out=rA2, in0=dst[:, :, :], scalar1=1.0 / (2 * math.pi), scalar2=None, op0=ALU.mult),
            [nm + "_a"], ["trA"])
        cop('dve', lambda e_: e_.tensor_copy(out=rBi, in_=rA2), ["trA"], ["trB"])
        cop('dve', lambda e_: e_.tensor_copy(out=rA2, in_=rBi), ["trB"], ["trA"])
        cop('dve', lambda e_, dst=dst: e_.scalar_tensor_tensor(out=dst[:, :, :], in0=rA2, scalar=-C1, in1=dst[:, :, :], op0=ALU.mult, op1=ALU.add),
            ["trA", nm + "_a"], [nm + "_a"])
        cop('dve', lambda e_, dst=dst: e_.scalar_tensor_tensor(out=dst[:, :, :], in0=rA2, scalar=-C2, in1=dst[:, :, :], op0=ALU.mult, op1=ALU.add),
            ["trA", nm + "_a"], [nm + "_a"])
        cop('dve', lambda e_, dst=dst: e_.tensor_scalar(out=dst[:, :, :], in0=dst[:, :, :], scalar1=-math.pi, scalar2=math.pi, op0=ALU.max, op1=ALU.min),
            [nm + "_a"], [nm + "_a"])
        cop('act', lambda e_, dst=dst: e_.activation(out=dst[:, :, :], in_=dst[:, :, :], func=AF.Sin), [nm + "_a"], [nm])

    def norm_tile(l, t, gbuf, gname, k=0):
        hb = h_bf if k == 0 else junk
        ss_, rs_ = (ss, rs) if k == 0 else (ss1, rs1)
        hn_, sn_, rn_ = ("h_bf", "ss", "rs") if k == 0 else (("h_bf", 1), ("ss", 1), ("rs", 1))
        cop('dve', lambda e_: e_.memset(ss_[:, :], 0.0), [], [sn_])
        cop('act', lambda e_: e_.activation(out=hb[:, :], in_=x_sb[:, t, :], func=AF.Square, accum_out=ss_[:, 0:1]),
            [("x", t), sn_], [sn_, hn_])
        cop('act', lambda e_: e_.activation(out=rs_[:, :], in_=ss_[:, :], func=AF.Sqrt, bias=EPS, scale=1.0 / D), [sn_], [rn_])
        cop('dve', lambda e_: e_.reciprocal(out=rs_[:, :], in_=rs_[:, :]), [rn_], [rn_])
        cop('dve', lambda e_: e_.scalar_tensor_tensor(out=hb[:, :], in0=x_sb[:, t, :], scalar=rs_[:, 0:1],
                                                      in1=gbuf[:, :], op0=ALU.mult, op1=ALU.mult),
            [rn_, ("x", t), gname], [hn_])

    def final_tile(t):
        yi = t % 2
        ss_, rs_ = (ss, rs) if yi == 0 else (ss1, rs1)
        sn_, rn_ = ("ss", "rs") if yi == 0 else (("ss", 1), ("rs", 1))
        cop('dve', lambda e_: e_.memset(ss_[:, :], 0.0), [], [sn_])
        cop('act', lambda e_: e_.activation(out=yout[yi][:, :], in_=x_sb[:, t, :], func=AF.Square, accum_out=ss_[:, 0:1]),
            [("x", t), sn_], [sn_, ("yout", yi)])
        cop('act', lambda e_: e_.activation(out=rs_[:, :], in_=ss_[:, :], func=AF.Sqrt, bias=EPS, scale=1.0 / D), [sn_], [rn_])
        cop('dve', lambda e_: e_.reciprocal(out=rs_[:, :], in_=rs_[:, :]), [rn_], [rn_])
        cop('dve', lambda e_: e_.scalar_tensor_tensor(out=yout[yi][:, :], in0=x_sb[:, t, :], scalar=rs_[:, 0:1], in1=gb1[:, :],
                                                      op0=ALU.mult, op1=ALU.mult), [rn_, ("x", t), "gb1"], [("yout", yi)])
        dma('sp', out_d[t * P:(t + 1) * P, :], yout[yi][:, :], [("yout", yi)], [("out", t)])

    def transpose_to(src_ap_fn, nblk, src_names, dst_ap, dst_names, rows=P, width=P, evac='act'):
        pi = nxt('pt', 2)
        ptt = pt[pi]

        def f(e_):
            ins = None
            for i in range(nblk):
                ins = e_.transpose(out=ptt[0:width, i * P:(i + 1) * P], in_=src_ap_fn(i), identity=ident[:, :])
            return ins
        cop('pe', f, list(src_names) + ["ident"], [("pt", pi)])
        src_v = ptt[0:width, 0:nblk * P].rearrange("p (k n) -> p k n", k=nblk) if len(dst_ap.shape) == 3 else ptt[0:width, 0:nblk * P]
        if evac == 'act':
            cop('act', lambda e_: e_.activation(out=dst_ap, in_=src_v, func=AF.Copy), [("pt", pi)], list(dst_names))
        else:
            cop('dve', lambda e_: e_.tensor_copy(out=dst_ap, in_=src_v), [("pt", pi)], list(dst_names))

    def load_w(src_ap, dst_ap, name):
        dma('pool', dst_ap, src_ap, [], [name])

    def mm_group(out_ap, pairs, r, w):
        n = len(pairs)

        def f(e_):
            ins = None
            for i, (a, b) in enumerate(pairs):
                ins = e_.matmul(out_ap, a, b, start=(i == 0), stop=(i == n - 1))
            return ins
        cop('pe', f, r, w)

    def rotary(psrc, H, t, dst_ap, r, w, scale_ap=None):
        zi = nxt('zf', 2)
        z3 = zf[zi][:, 0:H * 64].rearrange("p (h d) -> p h d", h=H)
        cop('act', lambda e_: e_.activation(out=zf[zi][:, 0:H * 64], in_=psrc, func=AF.Copy), list(r), [("zf", zi)])
        A3 = rA[:, 0:H * 64].rearrange("p (h d) -> p h d", h=H)
        B3 = rB[:, 0:H * 64].rearrange("p (h d) -> p h d", h=H)
        C3 = rC[:, 0:H * 64].rearrange("p (h d) -> p h d", h=H)
        cb_ = cos_t[:, t, :].unsqueeze(1).to_broadcast([P, H, 32])
        sb_ = sin_t[:, t, :].unsqueeze(1).to_broadcast([P, H, 32])
        d3 = dst_ap.rearrange("p (h d) -> p h d", h=H)
        lo, hi = slice(0, 32), slice(32, 64)
        for eng, me, other, opc, tag in (('dve', lo, hi, ALU.subtract, "0"), ('pool', hi, lo, ALU.add, "1")):
            cop(eng, lambda e_, me=me: e_.tensor_tensor(out=A3[:, :, me], in0=z3[:, :, me], in1=cb_, op=ALU.mult), [("zf", zi), "cos"], ["rA" + tag])
            cop(eng, lambda e_, me=me, other=other: e_.tensor_tensor(out=B3[:, :, me], in0=z3[:, :, other], in1=sb_, op=ALU.mult), [("zf", zi), "sin"], ["rB" + tag])
            if scale_ap is None:
                cop(eng, lambda e_, me=me, opc=opc: e_.tensor_tensor(out=d3[:, :, me], in0=A3[:, :, me], in1=B3[:, :, me], op=opc),
                    ["rA" + tag, "rB" + tag], [(w[0], tag)])
            else:
                sc = scale_ap.unsqueeze(2).to_broadcast([P, H, 32])
                cop(eng, lambda e_, me=me, opc=opc: e_.tensor_tensor(out=C3[:, :, me], in0=A3[:, :, me], in1=B3[:, :, me], op=opc),
                    ["rA" + tag, "rB" + tag], ["rC" + tag])
                cop(eng, lambda e_, me=me, sc=sc: e_.tensor_tensor(out=d3[:, :, me], in0=C3[:, :, me], in1=sc, op=ALU.mult),
                    ["rC" + tag, "dec8"], [(w[0], tag)])

    def win_chunk(l, c0, width=512):
        return w_in_d[l, :, c0:c0 + width].rearrange("(k p) n -> p k n", p=P)

    def layer(l):
        L = f"L{l}"
        dma('sp', gb1[:, :], g1_d[l], [], ["gb1"])
        dma('sp', gb2[:, :], g2_d[l], [], ["gb2"])
        dma('sp', bgate[:, :], bgate_d[l], [], ["bgate"])
        dma('sp', pscale[:, :], pscale_d[l], [], ["pscale"])
        dma('sp', convw[:, :], convw_d[l], [], ["convw"])
        dma('sp', convb[:, :], convb_d[l], [], ["convb"])
        dma('sp', plblk[:, :, :], plblk_d[l].rearrange("p (b n) -> p b n", b=2), [], ["plblk"])

        cop('act', lambda e_: e_.activation(out=ss[:, 0:1], in_=ident[:, 0:1], func=AF.Copy), ALLBUF(), ALLBUF())
        cop('dve', lambda e_: e_.memset(Sst[:, :], 0.0), ALLBUF(), ALLBUF())
        cop('pool', lambda e_: e_.memset(rs[:, :], 0.0), ALLBUF(), ALLBUF())
        for i in range(2):
            cop('dve', lambda e_, i=i: e_.memset(vst[i][:, :, :, :], 1.0), [], [("vst", i)])
        cop('dve', lambda e_: e_.memset(zt[:, :], 0.0), [], ["zt"])
        dma('sp', SX[l][64:128, 0:512], zt[64:128, :], ["zt"], [("SX", L, 2)])

        mark(f'A0_{l}')
        def hb_copies():
            kvnames = [("KVd", L, t, c) for t in range(NT) for c in (0, KVW, 2 * KVW, 256, KVW + 256, 2 * KVW + 256)]
            KV16 = KVd[l].rearrange("(m d) c -> d m c", d=16)
            KV4 = KVd[l].rearrange("(m d) c -> d m c", d=4)
            for r_ in range(16):
                dma('sp', HB[l][r_ * P:(r_ + 1) * P, :], KV16[r_, :, 2 * KVW:3 * KVW], kvnames, [("HB", L, r_)])
            for r_ in range(4):
                dma('sp', HB[l][(16 + r_) * P:(17 + r_) * P, :], KV4[r_, 384:512, KVW:2 * KVW], kvnames, [("HB", L, 16 + r_)])
            dma('sp', HB[l][20 * P:21 * P, :], KVd[l][T - P:T, 0:KVW], kvnames, [("HB", L, 20)])

        def kv_collectives():
            for c in range(6):
                nr = 512 if c < 5 else 128
                reg(("HBall", L, c))
                op('pool', lambda e_, l=l, c=c, nr=nr: e_.collective_compute(
                    "AllGather", ALU.bypass, replica_groups=[[0, 1, 2, 3], [4, 5, 6, 7]],
                    ins=[HB[l][c * 512:c * 512 + nr, :]], outs=[HBall[l][c][:, :]], dma_qos="P3"),
                   r=[("HB", L, bi) for bi in range(c * 4, min(c * 4 + 4, 21))], w=[("HBall", L, c)], kind='cc')

        chunks = [(3072, 'av01'), (3584, 'av2pu'), (2048, 'aq2k0'), (2560, 'ak12'), (0, 'rqrk'), (1536, 'aq01'), (512, 'rv')]
        deferred = []

        def flush_deferred():
            while deferred:
                deferred.pop(0)()

        def proc_tile(c0, kind, wi, t):
            pi = nxt('ps', 4)
            mm_group(ps[pi][:, :], [(hT_all[:, k, t * P:(t + 1) * P], wb[wi][:, k, :]) for k in range(8)],
                     [("hT", t), ("wb", wi)], [("ps", pi)])
            flush_deferred()
            rows = slice(t * P, (t + 1) * P)
            if kind == 'rqrk':
                oi = nxt('ob', 4)
                rotary(ps[pi][:, :], 8, t, obf[oi][:, :], [("ps", pi)], [("obf", oi)], scale_ap=dec8[:, :])
                cop('act', lambda e_, oi=oi, t=t: e_.activation(out=ktok[:, t, :], in_=obf[oi][:, 256:512], func=AF.Copy),
                    [(("obf", oi), "0"), (("obf", oi), "1")], [("ktok", t)])

                def later(oi=oi, t=t):
                    transpose_to(lambda i: obf[oi][:, i * 64:(i + 1) * 64], 4, [(("obf", oi), "0"), (("obf", oi), "1")],
                                 qT[:, :, t * P:(t + 1) * P], [("qT", t)], width=64)
                    transpose_to(lambda i: obf[oi][:, 256 + i * 64:256 + (i + 1) * 64], 4, [(("obf", oi), "0"), (("obf", oi), "1")],
                                 kT[:, :, t * P:(t + 1) * P], [("kT", t)], width=64, evac='dve')
                deferred.append(later)
            elif kind == 'rv':
                ri = t % 4
                cop('act', lambda e_, pi=pi, ri=ri: e_.activation(out=rvt[ri][:, 0:512], in_=ps[pi][:, :], func=AF.Copy),
                    [("ps", pi)], [("rvt", ri)])
                cop('act', lambda e_, ri=ri, t=t: e_.activation(out=rvt[ri][:, 512:768], in_=ktok[:, t, :], func=AF.Copy),
                    [("ktok", t)], [("rvt", ri)])
                def later(ri=ri, t=t):
                    def fS(e_):
                        ins = None
                        for h in range(4):
                            ins = e_.matmul(pS2[0:64, h * P:(h + 1) * P], ktok[:, t, h * 64:(h + 1) * 64],
                                            rvt[ri][:, h * P:(h + 1) * P], start=True, stop=True)
                        return ins
                    cop('pe', fS, [("rvt", ri), ("ktok", t)], ["pS2"])
                    cop('dve', lambda e_: e_.tensor_tensor(out=Sst[:, :], in0=Sst[:, :], in1=pS2[0:64, 0:512], op=ALU.add),
                        ["pS2", "Sst"], ["Sst"])
                    cop('dve', lambda e_: e_.tensor_tensor(out=Sst[:, :], in0=Sst[:, :], in1=gtab[:, :], op=ALU.mult),
                        ["Sst", "gtab"], ["Sst"])
                deferred.append(later)
            elif kind in ('aq01', 'aq2k0', 'ak12'):
                oi = nxt('ob', 4)
                rotary(ps[pi][:, :], 8, t, obf[oi][:, :], [("ps", pi)], [("obf", oi)])
                dsts = {'aq01': [(Qd[l], 0, "Qd"), (Qd[l], 256, "Qd")],
                        'aq2k0': [(Qd[l], 512, "Qd"), (KVd[l], 0, "KVd")],
                        'ak12': [(KVd[l], KVW, "KVd"), (KVd[l], 2 * KVW, "KVd")]}[kind]
                for hh, (dt_, c, nm) in enumerate(dsts):
                    dma('sp', dt_[rows, c:c + 256], obf[oi][:, hh * 256:(hh + 1) * 256],
                        [(("obf", oi), "0"), (("obf", oi), "1")], [(nm, L, t, c)])
            elif kind == 'av01':
                vi = nxt('ob', 2)
                cop('act', lambda e_, pi=pi, vi=vi: e_.activation(
                    out=vst[vi][:, :, :, 0:64], in_=ps[pi][:, :].rearrange("p (g h d) -> p g h d", g=2, h=4), func=AF.Copy),
                    [("ps", pi)], [("vst", vi)])
                for g in range(2):
                    dma('sp', KVd[l][rows, g * KVW + 256:(g + 1) * KVW], vst[vi][:, g, :, :].rearrange("p h d -> p (h d)"),
                        [("vst", vi)], [("KVd", L, t, g * KVW + 256)])
            else:
                vi = nxt('ob', 2)
                cop('act', lambda e_, pi=pi, vi=vi: e_.activation(
                    out=vst[vi][:, 0, :, 0:64], in_=ps[pi][:, 0:256].rearrange("p (h d) -> p h d", h=4), func=AF.Copy),
                    [("ps", pi)], [("vst", vi)])
                dma('sp', KVd[l][rows, 2 * KVW + 256:3 * KVW], vst[vi][:, 0, :, :].rearrange("p h d -> p (h d)"),
                    [("vst", vi)], [("KVd", L, t, 2 * KVW + 256)])
                oi = nxt('ob', 4)
                cop('act', lambda e_, pi=pi, oi=oi: e_.activation(out=obf[oi][:, 0:256], in_=ps[pi][:, 256:512], func=AF.Copy),
                    [("ps", pi)], [(("obf", oi), "0"), (("obf", oi), "1")])
                dma('sp', RVd[l][rows, 768:1024], obf[oi][:, 0:256], [(("obf", oi), "0"), (("obf", oi), "1")], [("RVd", L, t, 768)])
                if t == NT - 1:
                    cop('dve', lambda e_, pi=pi: e_.tensor_copy(out=puf[:, :], in_=ps[pi][:, 256:512]), [("ps", pi)], ["puf"])
                    dma('sp', SX[l][:, 512:768], puf[:, :], ["puf"], [("SX", L, 1)])
            if kind == 'rv':
                dma('sp', RVd[l][rows, 0:768], rvt[t % 4][:, 0:768], [("rvt", t % 4)], [("RVd", L, t, 0)])

        wis = [nxt('wb', 2) for _ in chunks]
        load_w(win_chunk(l, chunks[0][0]), wb[wis[0]][:, :, :], ("wb", wis[0]))
        for ci_, (c0, kind) in enumerate(chunks):
            wi = wis[ci_]
            if ci_ + 1 < len(chunks):
                load_w(win_chunk(l, chunks[ci_ + 1][0]), wb[wis[ci_ + 1]][:, :, :], ("wb", wis[ci_ + 1]))
            if kind == 'rqrk':
                hb_copies()
            if kind == 'aq01':
                kv_collectives()
            if ci_ == 0:
                norm_tile(l, 0, gb1, "gb1", k=0)
            for t in range(NT):
                if ci_ == 0:
                    if t + 1 < NT:
                        norm_tile(l, t + 1, gb1, "gb1", k=(t + 1) % 2)
                    hbt = h_bf if t % 2 == 0 else junk
                    hbn = "h_bf" if t % 2 == 0 else ("h_bf", 1)
                    transpose_to(lambda i, hbt=hbt: hbt[:, i * P:(i + 1) * P], 8, [hbn], hT_all[:, :, t * P:(t + 1) * P], [("hT", t)])
                    if t % 4 == 3:
                        g4 = t // 4
                        dma('sp', hTd[l][:, :, g4 * 512:(g4 + 1) * 512], hT_all[:, :, g4 * 512:(g4 + 1) * 512],
                            [("hT", tq) for tq in range(g4 * 4, g4 * 4 + 4)], [("hTd", L, g4)])
                proc_tile(c0, kind, wi, t)
            flush_deferred()
            if kind == 'rv':
                dma('sp', SX[l][0:64, 0:512], Sst[:, :], ["Sst"], [("SX", L, 0)])

        mark(f'A1_{l}')
        reg(("SXall", L))
        op('pool', lambda e_, l=l: e_.collective_compute("AllGather", ALU.bypass, replica_groups=[[0, 1, 2, 3], [4, 5, 6, 7]],
                                                          ins=[SX[l][:, :]], outs=[SXall[l][:, :]]),
           r=[("SX", L, 0), ("SX", L, 1), ("SX", L, 2)], w=[("SXall", L)], kind='cc')

        fullbar(skip=("HBall", "SXall", "HB", "SX"))

        mark(f'X1_{l}')
        blocks = []
        for g in range(2):
            nb_ = 16 // GD[g]
            for r_ in range(GD[g]):
                blocks.append((g, r_, 0, 'warm'))
                for b in range(1, nb_):
                    blocks.append((g, r_, b, 'blk'))
                blocks.append((g, r_, 0, 'blk'))
        for r_ in range(16):
            blocks.append((2, r_, 0, 'blk'))

        def att_stage0(i):
            g, r_, b, mode = blocks[i]
            d = GD[g]
            ci, k3 = i % 3, i % 6
            KVv = KVd[l].rearrange("(n d) c -> d n c", d=d)
            tiles = list(range(b * d, (b + 1) * d))
            ms = slice(b * P, (b + 1) * P)
            kn = [("KVd", L, t, g * KVW) for t in tiles] + [("KVd", L, t, g * KVW + 256) for t in tiles]
            if mode != 'warm':
                Qv = Qd[l].rearrange("(n d) c -> d n c", d=d)
                qn = [("Qd", L, t, g * 256) for t in tiles]
                dma('sp', qblk[ci][:, :], Qv[r_, ms, g * 256:(g + 1) * 256], qn, [("qblk", ci)])
            dma('sp', kvo[k3][:, :], KVv[r_, ms, g * KVW:(g + 1) * KVW], kn, [("kvo", k3)])
            if mode != 'warm' and b == 0:
                hs_ = i % 2
                bi = {0: 20, 1: 16 + r_, 2: r_}[g]
                hc, hoff = bi // 4, (bi % 4) * P
                hnr = 512 if hc < 5 else 128
                for s_ in range(3):
                    dma('sp', kvc[hs_][s_][:, :], HBall[l][hc][s_ * hnr + hoff:s_ * hnr + hoff + P, :], [("HBall", L, hc)], [("kvc", hs_, s_)])

        def att_stage1a(i):
            g, r_, b, mode = blocks[i]
            d = GD[g]
            ci, k3, qi = i % 2, i % 6, i % 3
            if mode == 'warm':
                KVv = KVd[l].rearrange("(n d) c -> d n c", d=d)
                tiles = list(range(b * d, (b + 1) * d))
                ms = slice(b * P, (b + 1) * P)
                kn = [("KVd", L, t, g * KVW) for t in tiles] + [("KVd", L, t, g * KVW + 256) for t in tiles]
                transpose_to(lambda j: kvo[k3][:, j * 64:(j + 1) * 64], 4, [("kvo", k3)], KTo[k3][:, :, :], [("KTo", k3)], width=64, evac='dve')
                return
            Qv = Qd[l].rearrange("(n d) c -> d n c", d=d)
            KVv = KVd[l].rearrange("(n d) c -> d n c", d=d)
            tiles = list(range(b * d, (b + 1) * d))
            ms = slice(b * P, (b + 1) * P)
            qn = [("Qd", L, t, g * 256) for t in tiles]
            kn = [("KVd", L, t, g * KVW) for t in tiles] + [("KVd", L, t, g * KVW + 256) for t in tiles]
            transpose_to(lambda j: qblk[qi][:, j * 64:(j + 1) * 64], 4, [("qblk", qi)], QT[ci][:, :, :], [("QT", ci)], width=64)
            transpose_to(lambda j: kvo[k3][:, j * 64:(j + 1) * 64], 4, [("kvo", k3)], KTo[k3][:, :, :], [("KTo", k3)], width=64, evac='dve')
            sA, sB = (ps[0], ps[1]) if ci == 0 else (ps[2], ps[3])
            sAn, sBn = (("ps", 0), ("ps", 1)) if ci == 0 else (("ps", 2), ("ps", 3))
            pO = pS2[:, ci * 512:(ci + 1) * 512]
            pOn = ("pS2", ci)
            if b >= 1:
                pass
            else:
                hi_ = i % 2
                v3 = i % 3
                bi = {0: 20, 1: 16 + r_, 2: r_}[g]
                hc, hoff = bi // 4, (bi % 4) * P
                hnr = 512 if hc < 5 else 128
                hs_ = i % 2

                def fK(e_):
                    ins = None
                    for h in range(4):
                        for s_ in range(3):
                            ins = e_.matmul(pO[0:64, h * P:(h + 1) * P], kvc[hs_][s_][:, h * 64:(h + 1) * 64], selI[:, s_, :],
                                            start=(s_ == 0), stop=(s_ == 2))
                    return ins
                cop('pe', fK, [("kvc", hs_, 0), ("kvc", hs_, 1), ("kvc", hs_, 2), ("selI", 0), ("selI", 1), ("selI", 2)], [pOn])
                cop('act', lambda e_: e_.activation(out=KTh[hi_][:, :, :], in_=pO[0:64, :].rearrange("p (h n) -> p h n", h=4), func=AF.Copy),
                    [pOn], [("KTh", hi_)])

                def fV(e_):
                    ins = None
                    for s_ in range(3):
                        ins = e_.matmul(pO[:, 0:260], selI[:, s_, :], kvc[hs_][s_][:, 256:KVW], start=(s_ == 0), stop=(s_ == 2))
                    return ins
                cop('pe', fV, [("kvc", hs_, 0), ("kvc", hs_, 1), ("kvc", hs_, 2), ("selI", 0), ("selI", 1), ("selI", 2)], [pOn])
                cop('act', lambda e_: e_.activation(out=vprev[v3][:, :], in_=pO[:, 0:260], func=AF.Copy), [pOn], [("vprev", v3)])

        def att_stage1b(i):
            g, r_, b, mode = blocks[i]
            if mode == 'warm':
                return
            ci, k3 = i % 2, i % 6
            sA, sB = (ps[0], ps[1]) if ci == 0 else (ps[2], ps[3])
            sAn, sBn = (("ps", 0), ("ps", 1)) if ci == 0 else (("ps", 2), ("ps", 3))
            if b >= 1:
                p3 = (i - 1) % 6
                KTp, KTpn = KTo[p3], ("KTo", p3)
            else:
                KTp, KTpn = KTh[i % 2], ("KTh", i % 2)

            def fS(e_):
                ins = None
                for h in range(4):
                    dst = sA if h < 2 else sB
                    o = (h % 2) * 256
                    e_.matmul(dst[:, o:o + P], KTp[:, h, :], QT[ci][:, h, :], start=True, stop=True)
                    ins = e_.matmul(dst[:, o + P:o + 256], KTo[k3][:, h, :], QT[ci][:, h, :], start=True, stop=True)
                return ins
            cop('pe', fS, [KTpn, ("KTo", k3), ("QT", ci)], [sAn, sBn])
            cop('act', lambda e_: e_.activation(out=Pe[ci][:, 0:512], in_=sA[:, :], func=AF.Exp, scale=0.125), [sAn], [("Pe", ci, 0)])
            cop('act', lambda e_: e_.activation(out=Pe[ci][:, 512:1024], in_=sB[:, :], func=AF.Exp, scale=0.125), [sBn], [("Pe", ci, 1)])
            cop('dve', lambda e_: e_.tensor_tensor(out=Pm[ci][:, :].rearrange("p (h n) -> p h n", h=4),
                                                   in0=Pe[ci][:, :].rearrange("p (h n) -> p h n", h=4),
                                                   in1=mask2[:, :].unsqueeze(1).to_broadcast([P, 4, 256]), op=ALU.mult),
                [("Pe", ci, 0), ("Pe", ci, 1), "mask2"], [("Pm", ci)])

        def att_stage2(i):
            g, r_, b, mode = blocks[i]
            if mode == 'warm':
                return
            d = GD[g]
            ci, k3 = i % 2, i % 6
            Ov = Od[l].rearrange("(n d) c -> d n c", d=d)
            tiles = list(range(b * d, (b + 1) * d))
            ms = slice(b * P, (b + 1) * P)
            pO = pS2[:, ci * 512:(ci + 1) * 512]
            pOn = ("pS2", ci)
            if b >= 1:
                p3 = (i - 1) % 6
                vp, vpn = kvo[p3][:, 256:KVW], ("kvo", p3)
            else:
                vp, vpn = vprev[i % 3][:, :], ("vprev", i % 3)

            def fO(e_):
                ins = None
                for h in range(4):
                    e_.matmul(pO[:, h * 65:(h + 1) * 65], Pm[ci][:, h * 256:h * 256 + P], vp[:, h * 65:(h + 1) * 65], start=True, stop=False)
                    ins = e_.matmul(pO[:, h * 65:(h + 1) * 65], Pm[ci][:, h * 256 + P:(h + 1) * 256],
                                    kvo[k3][:, 256 + h * 65:256 + (h + 1) * 65], start=False, stop=True)
                return ins
            cop('pe', fO, [("Pm", ci), vpn, ("kvo", k3)], [pOn])
            cop('act', lambda e_: e_.activation(out=osb[ci][:, :], in_=pO[:, 0:260], func=AF.Copy), [pOn], [("osb", ci)])
            dma('pool', Ov[r_, ms, g * 260:(g + 1) * 260], osb[ci][:, :], [("osb", ci)], [("Od", L, g, t) for t in tiles])

        nblk = len(blocks)
        att_stage0(0)
        for i in range(nblk + 2):
            if i + 1 < nblk:
                att_stage0(i + 1)
            if i < nblk:
                att_stage1a(i)
            if 1 <= i <= nblk:
                att_stage1b(i - 1)
            if 2 <= i <= nblk + 1:
                att_stage2(i - 2)

        fullbar()
        pacc = ycp[0][:, 0:256]
        for s in range(3):
            dma('sp', sxc[:, :], SXall[l][s * P:(s + 1) * P, :], [("SXall", L)], ["sxc"])
            for h in range(4):
                hs = slice(h * P, (h + 1) * P)
                if s == 0:
                    cop('dve', lambda e_, hs=hs, h=h: e_.tensor_scalar(out=Sst[:, hs], in0=sxc[0:64, hs], scalar1=coef[:, h:h + 1],
                                                                       scalar2=None, op0=ALU.mult), ["sxc", "coef"], ["Sst"])
                else:
                    cop('dve', lambda e_, hs=hs, h=h, s=s: e_.scalar_tensor_tensor(
                        out=Sst[:, hs], in0=sxc[0:64, hs], scalar=coef[:, s * 4 + h:s * 4 + h + 1], in1=Sst[:, hs],
                        op0=ALU.mult, op1=ALU.add), ["sxc", "coef", "Sst"], ["Sst"])
            if s == 0:
                cop('dve', lambda e_: e_.tensor_scalar(out=pacc, in0=sxc[:, 512:768], scalar1=sel[:, 0:1], scalar2=None,
                                                       op0=ALU.mult), ["sxc", "sel"], [("ycp", 0)])
            elif s == 1:
                cop('dve', lambda e_: e_.scalar_tensor_tensor(out=pacc, in0=sxc[:, 512:768], scalar=sel[:, 1:2], in1=pacc,
                                                              op0=ALU.mult, op1=ALU.add), ["sxc", "sel", ("ycp", 0)], [("ycp", 0)])
            else:
                cop('dve', lambda e_: e_.scalar_tensor_tensor(out=puh[:, :], in0=sxc[:, 512:768], scalar=sel[:, 2:3], in1=pacc,
                                                              op0=ALU.mult, op1=ALU.add), ["sxc", "sel", ("ycp", 0)], ["puh"])
        cop('act', lambda e_: e_.activation(out=Sbf[:, :], in_=Sst[:, :], func=AF.Copy), ["Sst"], ["Sbf"])

        mark(f'ATT_{l}')
        for gi in range(4):
            wi = nxt('wb', 2)
            load_w(win_chunk(l, 1024), wb[wi][:, :, :], ("wb", wi))
            dma('sp', hTg[:, :, :], hTd[l][:, :, gi * 512:(gi + 1) * 512], [("hTd", L, gi)], [("hTg", tq) for tq in range(4)])

            def do_tile(tt, gi=gi, wi=wi):
                t = gi * 4 + tt
                tc = slice(tt * P, (tt + 1) * P)
                tg = slice(t * P, (t + 1) * P)
                bi_ = t % 2
                sg_, PT_, ycp_, ysq_, st4_, yrb_, Ot_, num_, rden_, yab_, ppb_, ppT_ = (sg[bi_], PT[bi_], ycp[bi_], ysq[bi_], st4[bi_], yrb[bi_],
                                                                                         Ot[bi_], num[bi_], rden[bi_], yab[bi_], ppb[bi_], ppT[bi_])
                ri = t % 2
                dma('sp', rvk[ri][:, 0:768], RVd[l][tg, 0:768], [("RVd", L, t, 0)], [("rvk", ri)])
                dma('sp', pus[t % 4][:, :], RVd[l][tg, 768:1024], [("RVd", L, t, 768)], [("pus", t % 4)])
                dma('sp', Ot_[:, :], Od[l][tg, :], [("Od", L, g, t) for g in range(3)], [("Ot", bi_)])
                yield
                pi = nxt('ps', 4)
                mm_group(ps[pi][:, :], [(hTg[:, k, tc], wb[wi][:, k, :]) for k in range(8)], [("hTg", tt), ("wb", wi)], [("ps", pi)])
                cop('act', lambda e_, pi=pi: e_.activation(out=sg_[:, :], in_=ps[pi][:, :], func=AF.Silu), [("ps", pi)], [("sg", bi_)])
                yield
                if t == 0: mark(f'B1_{l}')
                pi = nxt('ps', 4)

                def fR(e_, pi=pi, tg=tg):
                    ins = None
                    for h in range(4):
                        ins = e_.matmul(ps[pi][:, h * P:(h + 1) * P], kT[:, h, tg], qT[:, h, tg], start=True, stop=True)
                    return ins
                cop('pe', fR, [("kT", t), ("qT", t)], [("ps", pi)])
                cop('dve', lambda e_, pi=pi: e_.tensor_tensor(out=PT_[:, :].rearrange("p (h n) -> p h n", h=4),
                                                              in0=ps[pi][:, :].rearrange("p (h n) -> p h n", h=4),
                                                              in1=mask2[:, P:256].unsqueeze(1).to_broadcast([P, 4, P]), op=ALU.mult),
                    [("ps", pi), "mask2"], [("PT", bi_)])
                yield
                pi = nxt('ps', 4)

                def fY(e_, pi=pi, tg=tg, ri=ri):
                    ins = None
                    for h in range(4):
                        e_.matmul(ps[pi][:, h * P:(h + 1) * P], PT_[:, h * P:(h + 1) * P], rvk[ri][:, h * P:(h + 1) * P], start=True, stop=False)
                        ins = e_.matmul(ps[pi][:, h * P:(h + 1) * P], qT[:, h, tg], Sbf[:, h * P:(h + 1) * P], start=False, stop=True)
                    return ins
                cop('pe', fY, [("PT", bi_), ("rvk", ri), ("qT", t), "Sbf"], [("ps", pi)])
                cop('act', lambda e_, pi=pi: e_.activation(out=ycp_[:, :], in_=ps[pi][:, :], func=AF.Copy), [("ps", pi)], [("ycp", bi_)])
                yield
                if t == 0: mark(f'B2_{l}')
                def fU(e_, ri=ri):
                    ins = None
                    for h in range(4):
                        ins = e_.matmul(pS2[0:64, h * P:(h + 1) * P], rvk[ri][:, 512 + h * 64:512 + (h + 1) * 64],
                                        rvk[ri][:, h * P:(h + 1) * P], start=True, stop=True)
                    return ins
                cop('pe', fU, [("rvk", ri)], ["pS2"])
                cop('dve', lambda e_: e_.tensor_tensor(out=Sst[:, :], in0=Sst[:, :], in1=pS2[0:64, 0:512], op=ALU.add), ["pS2", "Sst"], ["Sst"])
                cop('dve', lambda e_: e_.tensor_tensor(out=Sst[:, :], in0=Sst[:, :], in1=gtab[:, :], op=ALU.mult), ["Sst", "gtab"], ["Sst"])
                cop('act', lambda e_: e_.activation(out=Sbf[:, :], in_=Sst[:, :], func=AF.Copy), ["Sst"], ["Sbf"])
                yield
                if t == 0: mark(f'B3_{l}')
                y3 = ycp_[:, :].rearrange("p (h n) -> p h n", h=4)
                cop('dve', lambda e_: e_.tensor_reduce(out=st4_[:, 0:4], in_=y3, axis=AX.X, op=ALU.add), [("ycp", bi_)], [("s1", bi_)])
                cop('act', lambda e_: e_.activation(out=ysq_[:, :], in_=ycp_[:, :], func=AF.Square), [("ycp", bi_)], [("ysq", bi_)])
                cop('dve', lambda e_: e_.tensor_reduce(out=st4_[:, 4:8], in_=ysq_[:, :].rearrange("p (h n) -> p h n", h=4), axis=AX.X, op=ALU.add),
                    [("ysq", bi_)], [("s2", bi_)])
                cop('dve', lambda e_: e_.tensor_scalar(out=st4_[:, 8:12], in0=st4_[:, 0:4], scalar1=1.0 / P, scalar2=None, op0=ALU.mult), [("s1", bi_)], [("mean", bi_)])
                cop('dve', lambda e_: e_.tensor_tensor(out=st4_[:, 12:16], in0=st4_[:, 8:12], in1=st4_[:, 8:12], op=ALU.mult), [("mean", bi_)], [("msq", bi_)])
                cop('dve', lambda e_: e_.scalar_tensor_tensor(out=st4_[:, 16:20], in0=st4_[:, 4:8], scalar=1.0 / P, in1=st4_[:, 12:16],
                                                              op0=ALU.mult, op1=ALU.subtract), [("s2", bi_), ("msq", bi_)], [("var", bi_)])
                yield
                cop('act', lambda e_: e_.activation(out=st4_[:, 16:20], in_=st4_[:, 16:20], func=AF.Sqrt, bias=EPS, scale=1.0), [("var", bi_)], [("rstd", bi_)])
                cop('dve', lambda e_: e_.reciprocal(out=st4_[:, 16:20], in_=st4_[:, 16:20]), [("rstd", bi_)], [("rstd", bi_)])
                cop('dve', lambda e_: e_.scalar_tensor_tensor(out=st4_[:, 20:24], in0=st4_[:, 8:12], scalar=-1.0, in1=st4_[:, 16:20],
                                                              op0=ALU.mult, op1=ALU.mult), [("mean", bi_), ("rstd", bi_)], [("nmr", bi_)])
                yield
                yq3 = ysq_[:, :].rearrange("p (h n) -> p h n", h=4)
                cop('dve', lambda e_: e_.tensor_tensor(out=yq3, in0=y3, in1=st4_[:, 16:20].unsqueeze(2).to_broadcast([P, 4, P]), op=ALU.mult),
                    [("ycp", bi_), ("rstd", bi_), ("s2", bi_)], [("ysq", bi_)])
                cop('dve', lambda e_: e_.tensor_tensor(out=yq3, in0=yq3, in1=st4_[:, 20:24].unsqueeze(2).to_broadcast([P, 4, P]), op=ALU.add),
                    [("ysq", bi_), ("nmr", bi_)], [("ysq", bi_)])
                cop('dve', lambda e_: e_.tensor_tensor(out=yrb_[:, :], in0=ysq_[:, :], in1=sg_[:, :], op=ALU.mult),
                    [("ysq", bi_), ("sg", bi_)], [("yrb", bi_), ("ysq", bi_)])
                yield
                transpose_to(lambda i: yrb_[:, i * P:(i + 1) * P], 4, [("yrb", bi_)], yretT[:, :, tc], [("yretT", tt)])
                yield
                if t == 0: mark(f'B4_{l}')
                O3 = Ot_[:, :].rearrange("p (g c) -> p g c", g=3)
                cop('dve', lambda e_: e_.tensor_tensor(out=num_[:, :], in0=O3[:, 0, :], in1=O3[:, 1, :], op=ALU.add), [("Ot", bi_)], [("num", bi_)])
                cop('dve', lambda e_: e_.tensor_tensor(out=num_[:, :], in0=num_[:, :], in1=O3[:, 2, :], op=ALU.add), [("Ot", bi_), ("num", bi_)], [("num", bi_)])
                n3 = num_[:, :].rearrange("p (h c) -> p h c", h=4)
                cop('dve', lambda e_: e_.reciprocal(out=rden_[:, :], in_=n3[:, :, 64]), [("num", bi_)], [("rden", bi_)])
                cop('dve', lambda e_: e_.tensor_tensor(out=yab_[:, :].rearrange("p (h c) -> p h c", h=4), in0=n3[:, :, 0:64],
                                                       in1=rden_[:, :].unsqueeze(2).to_broadcast([P, 4, 64]), op=ALU.mult),
                    [("num", bi_), ("rden", bi_)], [("yab", bi_)])
                yield
                transpose_to(lambda i: yab_[:, i * P:(i + 1) * P], 2, [("yab", bi_)], yattT[:, :, tc], [("yattT", tt)])
                yield
                if t == 0: mark(f'B5_{l}')
                pu_t = pus[t % 4][:, :]
                if t == 0:
                    pu_p, pun = puh[:, :], "puh"
                else:
                    pu_p, pun = pus[(t - 1) % 4][:, :], ("pus", (t - 1) % 4)
                pi = nxt('ps', 4)

                def fP(e_, pi=pi, pu_t=pu_t, pu_p=pu_p):
                    ins = None
                    for g in range(4):
                        e_.matmul(ps[pi][:, g * 64:(g + 1) * 64], bmat[:, g, :], pu_t[:, g * 64:(g + 1) * 64], start=True, stop=False)
                        ins = e_.matmul(ps[pi][:, g * 64:(g + 1) * 64], bmat[:, 4 + g, :], pu_p[:, g * 64:(g + 1) * 64], start=False, stop=True)
                    return ins
                cop('pe', fP, [("pus", t % 4), pun, "bmat"], [("ps", pi)])
                ic = 4 if t == 0 else 0
                pp3 = ppb_[:, :].rearrange("p (g c) -> p g c", g=4)
                cop('dve', lambda e_, pi=pi, ic=ic: e_.tensor_tensor(out=pp3, in0=ps[pi][:, 0:256].rearrange("p (g c) -> p g c", g=4),
                                                                   in1=invc[:, ic:ic + 4].unsqueeze(2).to_broadcast([P, 4, 64]), op=ALU.mult),
                    [("ps", pi), "invc"], [("ppb", bi_)])
                cop('dve', lambda e_, pu_t=pu_t: e_.tensor_tensor(out=ppb_[:, :], in0=ppb_[:, :], in1=pu_t, op=ALU.subtract),
                    [("ppb", bi_), ("pus", t % 4)], [("ppb", bi_)])
                yield
                transpose_to(lambda i: ppb_[:, i * P:(i + 1) * P], 2, [("ppb", bi_)], ppT_[:, :, :], [("ppT", bi_)])
                pi = nxt('ps', 4)

                def fL(e_, pi=pi):
                    ins = None
                    for bl in range(2):
                        ins = e_.matmul(ps[pi][:, bl * P:(bl + 1) * P], plblk[:, bl, :], ppT_[:, bl, :], start=True, stop=True)
                    return ins
                cop('pe', fL, ["plblk", ("ppT", bi_)], [("ps", pi)])
                for bl in range(2):
                    cop('act', lambda e_, pi=pi, bl=bl, tc=tc: e_.activation(out=ypoolT[:, bl, tc], in_=ps[pi][:, bl * P:(bl + 1) * P],
                                                                            func=AF.Copy, scale=pscale[:, bl:bl + 1]),
                        [("ps", pi), "pscale"], [("ypoolT", tt, bl)])
            pend_tiles = [0, 1, 2, 3]
            active = []

            def step(g_):
                try:
                    next(g_)
                    return True
                except StopIteration:
                    return False
            g0 = do_tile(pend_tiles.pop(0))
            step(g0)
            step(g0)
            active.append(g0)
            active.append(do_tile(pend_tiles.pop(0)))
            while active:
                for g_ in list(active):
                    if not step(g_):
                        active.remove(g_)
                        if pend_tiles:
                            active.append(do_tile(pend_tiles.pop(0)))
            if gi == 0: mark(f'B6_{l}')
            hTn = [("hTg", tt) for tt in range(4)]
            for j in range(8):
                js = slice(j * P, (j + 1) * P)
                for i in range(3):
                    load_w(w_in_d[l, :, 4096 + i * 1024 + j * P:4096 + i * 1024 + (j + 1) * P].rearrange("(k p) n -> p k n", p=P),
                           wg[:, :, i, :], ("wg", i))
                load_w(p_ret_d[l, :, js].rearrange("(k p) n -> p k n", p=P), wp[:, 0:4, :], "wp0")
                load_w(p_att_d[l, :, js].rearrange("(k p) n -> p k n", p=P), wp[:, 4:6, :], "wp1")
                load_w(p_pool_d[l, :, js].rearrange("(k p) n -> p k n", p=P), wp[:, 6:8, :], "wp2")
                pg = []
                for i in range(3):
                    pi = nxt('ps', 4)
                    mm_group(ps[pi][:, :], [(wg[:, k, i, :], hTg[:, k, :]) for k in range(8)], hTn + [("wg", i)], [("ps", pi)])
                    cop('act', lambda e_, pi=pi, i=i, j=j: e_.activation(out=sig[i], in_=ps[pi][:, :], func=AF.Sigmoid,
                                                                        bias=bgate[:, i * 8 + j:i * 8 + j + 1]),
                        [("ps", pi), "bgate"], [signm[i]])
                brs = [(yretT, 0, 4, [("yretT", tt) for tt in range(4)], "wp0"),
                       (yattT, 4, 2, [("yattT", tt) for tt in range(4)], "wp1"),
                       (ypoolT, 6, 2, [("ypoolT", tt, bl) for tt in range(4) for bl in range(2)], "wp2")]
                for i, (yT, k0, nk, names, wn) in enumerate(brs):
                    pi = nxt('ps', 4)
                    mm_group(ps[pi][:, :], [(wp[:, k0 + k, :], yT[:, k, :]) for k in range(nk)], names + [wn], [("ps", pi)])
                    dst = m0 if i == 0 else m1
                    dn = ("ycp", 0) if i == 0 else ("ysq", 0)
                    cop('dve', lambda e_, pi=pi, i=i, dst=dst: e_.tensor_tensor(out=dst[:, :], in0=ps[pi][:, :], in1=sig[i], op=ALU.mult),
                        [("ps", pi), signm[i]], [dn])
                    if i == 1:
                        cop('dve', lambda e_: e_.tensor_tensor(out=m0[:, :], in0=m0[:, :], in1=m1[:, :], op=ALU.add), [("ycp", 0), ("ysq", 0)], [("ycp", 0)])
                    if i == 2:
                        cop('dve', lambda e_, j=j: e_.tensor_tensor(out=mT[:, j, :], in0=m0[:, :], in1=m1[:, :], op=ALU.add), [("ycp", 0), ("ysq", 0)], [("mT", j)])
            if gi == 0: mark(f'B7_{l}')
            mTn = [("mT", j) for j in range(8)]
            for half in range(2):
                wi2 = nxt('wb', 2)
                load_w(w_o_d[l, :, half * 512:(half + 1) * 512].rearrange("(k p) n -> p k n", p=P), wb[wi2][:, :, :], ("wb", wi2))
                for tt in range(4):
                    t = gi * 4 + tt
                    tc = slice(tt * P, (tt + 1) * P)
                    pi = nxt('ps', 4)
                    mm_group(ps[pi][:, :], [(mT[:, k, tc], wb[wi2][:, k, :]) for k in range(8)], mTn + [("wb", wi2)], [("ps", pi)])
                    cop('dve', lambda e_, pi=pi, t=t, half=half: e_.tensor_tensor(out=x_sb[:, t, half * 512:(half + 1) * 512],
                                                                                  in0=x_sb[:, t, half * 512:(half + 1) * 512],
                                                                                  in1=ps[pi][:, :], op=ALU.add),
                        [("ps", pi), ("x", t)], [("x", t)])

        mark(f'B_{l}')
        norm_tile(l, NT - 1, gb2, "gb2")
        transpose_to(lambda i: h_bf[:, i * P:(i + 1) * P], 8, ["h_bf"], hTg[:, :, 0:P], [("hTg", 0)])
        cop('act', lambda e_: e_.activation(out=cxs[:, :].rearrange("p (k n) -> p k n", k=8), in_=hTg[:, :, P - 2:P], func=AF.Copy),
            [("hTg", 0)], ["cxs"])
        dma('sp', CX[l][:, :], cxs[:, :], ["cxs"], [("CX", L)])
        reg(("CXall", L))
        op('pool', lambda e_, l=l: e_.collective_compute("AllGather", ALU.bypass, replica_groups=[[0, 1, 2, 3], [4, 5, 6, 7]],
                                                          ins=[CX[l][:, :]], outs=[CXall[l][:, :]]),
           r=[("CX", L)], w=[("CXall", L)], kind='cc')
        fullbar(skip=("CX", "CXall"))

        def halo_select():
            for s_ in range(3):
                dma('sp', cxc[s_][:, :], CXall[l][s_ * P:(s_ + 1) * P, :], [("CXall", L)], [("cxc", s_)])
            cop('dve', lambda e_: e_.tensor_scalar(out=cxf[:, :], in0=cxc[0][:, :], scalar1=sel[:, 0:1], scalar2=None, op0=ALU.mult),
                [("cxc", 0), "sel"], ["cxf"])
            for s_ in (1, 2):
                cop('dve', lambda e_, s_=s_: e_.scalar_tensor_tensor(out=cxf[:, :], in0=cxc[s_][:, :], scalar=sel[:, s_:s_ + 1], in1=cxf[:, :],
                                                                     op0=ALU.mult, op1=ALU.add), [("cxc", s_), "sel", "cxf"], ["cxf"])
            cop('dve', lambda e_: e_.tensor_copy(out=h2T[:, :, 0:2], in_=cxf[:, :].rearrange("p (k n) -> p k n", k=8)), ["cxf"], ["h2halo"])

        mark(f'X2_{l}')
        if l == DEPTH - 1:
            dma('sp', gb1[:, :], gf_d, [], ["gb1"])
        wdn = [("wd", q) for q in range(11)]
        def c_norm_tile(gi, tt):
            norm_tile(l, gi * 4 + tt, gb2, "gb2", k=tt % 2)

        def c_transp(gi, tt):
            hbt = h_bf if tt % 2 == 0 else junk
            hbn = "h_bf" if tt % 2 == 0 else ("h_bf", 1)
            transpose_to(lambda i: hbt[:, i * P:(i + 1) * P], 8, [hbn], h2T[:, :, 2 + tt * P:2 + (tt + 1) * P], [("h2T", tt)])

        corder = [1, 2, 3, 0]
        norm_tile(l, 3, gb2, "gb2", k=1)
        c_transp(0, 3)
        cop('dve', lambda e_: e_.tensor_copy(out=h2T[:, :, 0:2], in_=h2T[:, :, 512:514]), [("h2T", 3)], ["h2halo"])
        c_norm_tile(corder[0], 0)
        for tt in range(4):
            if tt < 3:
                c_norm_tile(corder[0], tt + 1)
            c_transp(corder[0], tt)
        for cidx, gi in enumerate(corder):
            gnext = corder[cidx + 1] if cidx + 1 < 4 else None
            h2n = [("h2T", tt) for tt in range(4)] + ["h2halo"]
            for q in range(11):
                ui = q % 2
                load_w(w_up_d[l, :, q * 256:(q + 1) * 256].rearrange("(k p) n -> p k n", p=P), wua[ui][:, :, :], ("wua", ui))
                load_w(w_up_d[l, :, DFF + q * 256:DFF + (q + 1) * 256].rearrange("(k p) n -> p k n", p=P), wub[ui][:, :, :], ("wub", ui))
                if cidx == 0:
                    load_w(w_down_d[l, q * 256:(q + 1) * 256, :].rearrange("(k p) n -> p k n", p=P), wd[:, 2 * q:2 * q + 2, :], ("wd", q))
                for jj in range(2):
                    j = 2 * q + jj
                    cs = slice(jj * P, (jj + 1) * P)
                    pa = nxt('ps', 4)
                    mm_group(ps[pa][:, :], [(wua[ui][:, k, cs], h2T[:, k, 2:514]) for k in range(8)], h2n + [("wua", ui)], [("ps", pa)])
                    pb = nxt('ps', 4)
                    mm_group(ps[pb][:, :], [(wub[ui][:, k, cs], h2T[:, k, 2:514]) for k in range(8)], h2n + [("wub", ui)], [("ps", pb)])
                    mm_group(pS2[:, 0:2], [(wua[ui][:, k, cs], h2T[:, k, 0:2]) for k in range(8)], h2n + [("wua", ui)], ["pS2a"])
                    mm_group(pS2[:, 512:514], [(wub[ui][:, k, cs], h2T[:, k, 0:2]) for k in range(8)], h2n + [("wub", ui)], ["pS2b"])
                    for (ub, pp_, hoff, hn, nm, jc, cc) in ((uba, pa, 0, "pS2a", "uba", j, ca), (ubb, pb, 512, "pS2b", "ubb", 22 + j, cb)):
                        cop('act', lambda e_, ub=ub, pp_=pp_: e_.activation(out=ub[:, 2:514], in_=ps[pp_][:, :], func=AF.Copy),
                            [("ps", pp_)], [nm + "m"])
                        cop('act', lambda e_, ub=ub, hoff=hoff: e_.activation(out=ub[:, 0:2], in_=pS2[:, hoff:hoff + 2], func=AF.Copy),
                            [hn], [nm + "h"])
                        cop('act', lambda e_, ub=ub, jc=jc, cc=cc: e_.activation(out=cc[:, :], in_=ub[:, 2:514], func=AF.Identity,
                                                                                bias=convb[:, jc:jc + 1], scale=convw[:, 88 + jc:89 + jc]),
                            [nm + "m", "convw", "convb"], [nm + "c"])
                        cop('dve', lambda e_, ub=ub, jc=jc, cc=cc: e_.scalar_tensor_tensor(out=cc[:, :], in0=ub[:, 1:513], scalar=convw[:, 44 + jc:45 + jc],
                                                                                          in1=cc[:, :], op0=ALU.mult, op1=ALU.add),
                            [nm + "m", nm + "h", nm + "c", "convw"], [nm + "c"])
                        cop('dve', lambda e_, ub=ub, jc=jc, cc=cc: e_.scalar_tensor_tensor(out=cc[:, :], in0=ub[:, 0:512], scalar=convw[:, jc:jc + 1],
                                                                                          in1=cc[:, :], op0=ALU.mult, op1=ALU.add),
                            [nm + "m", nm + "h", nm + "c", "convw"], [nm + "c"])
                    cop('act', lambda e_: e_.activation(out=sa[:, :], in_=ca[:, :], func=AF.Silu), ["ubac"], ["sa"])
                    cop('dve', lambda e_, j=j: e_.tensor_tensor(out=gT[:, j, :], in0=sa[:, :], in1=cb[:, :], op=ALU.mult),
                        ["sa", "ubbc"], [("gT", j)])
            gTn = [("gT", j) for j in range(22)]
            if gnext is not None:
                if gnext == 0:
                    halo_select()
                else:
                    cop('dve', lambda e_: e_.tensor_copy(out=h2T[:, :, 0:2], in_=h2T[:, :, 512:514]), [("h2T", 3)], ["h2halo"])
                c_norm_tile(gnext, 0)
            for tt in range(4):
                t = gi * 4 + tt
                tc = slice(tt * P, (tt + 1) * P)
                for half in range(2):
                    pi = nxt('ps', 4)
                    mm_group(ps[pi][:, :], [(gT[:, j, tc], wd[:, j, half * 512:(half + 1) * 512]) for j in range(22)], gTn + wdn, [("ps", pi)])
                    cop('dve', lambda e_, pi=pi, t=t, half=half: e_.tensor_tensor(out=x_sb[:, t, half * 512:(half + 1) * 512],
                                                                                  in0=x_sb[:, t, half * 512:(half + 1) * 512],
                                                                                  in1=ps[pi][:, :], op=ALU.add),
                        [("ps", pi), ("x", t)], [("x", t)])
                if gnext is not None:
                    if tt < 3:
                        c_norm_tile(gnext, tt + 1)
                    c_transp(gnext, tt)
                if l == DEPTH - 1:
                    final_tile(t)
        mark(f'C_{l}')
        fullbar()

    try:
        for l in range(DEPTH):
            layer(l)
    except _Stop:
        pass

    cop('act', lambda e_: e_.activation(out=ss[:, 0:1], in_=ident[:, 0:1], func=AF.Copy), ALLBUF(), ALLBUF())

    S.finalize()
    sem_ids = set()
    for e in ENG:
        for rec in S.ops[e]:
            if 'sv' in rec:
                sem_ids.add(rec['sv'][0])
    sems = {}
    ctxs = []
    for sid in sorted(sem_ids, key=str):
        nm = "s_" + "_".join(str(x) for x in sid)
        cm = nc.semaphore(nm)
        sems[sid] = cm.__enter__()
        ctxs.append(cm)
    with nc.Block() as block:
        @block.tensor
        def _(eng):
            S.emit('pe', eng, sems)

        @block.scalar
        def _(eng):
            S.emit('act', eng, sems)

        @block.vector
        def _(eng):
            S.emit('dve', eng, sems)

        @block.gpsimd
        def _(eng):
            S.emit('pool', eng, sems)

        @block.sync
        def _(eng):
            S.emit('sp', eng, sems)
    for cm in reversed(ctxs):
        cm.__exit__(None, None, None)
    return nc


def _host_consts(rank):
    bf = ml_dtypes.bfloat16
    c = {}
    c["ident"] = np.eye(P, dtype=np.float32).astype(bf)
    j = np.arange(P)[:, None]
    i = np.arange(P)[None, :]
    mprev = (j >= i).astype(np.float32)
    mown = (j <= i).astype(np.float32)
    c["mask2"] = np.concatenate([mprev, mown], axis=1).astype(bf)
    inv = (10000.0 ** (-(np.arange(32, dtype=np.float32) / np.float32(32)))).astype(np.float32)
    c["invf"] = np.tile(inv[None, :], (P, 1)).astype(np.float32)
    gam = 1.0 - 2.0 ** (-5.0 - np.arange(4, dtype=np.float64))
    lg = np.log(gam)
    p = np.arange(P, dtype=np.float64)[:, None]
    qdec = np.exp(lg[None, :] * (p + 1.0))
    kdec = np.exp(-lg[None, :] * (p + 1.0)) * (64.0 ** -0.5)
    c["dec8"] = np.concatenate([qdec, kdec], axis=1).astype(np.float32)
    G = np.exp(lg * 128.0)
    c["gtab"] = np.tile(np.repeat(G, P)[None, :], (64, 1)).astype(np.float32)
    bm = np.zeros((P, 8, P), np.float32)
    tp = np.arange(P)[:, None]
    tt = np.arange(P)[None, :]
    for g, w in enumerate((2, 4, 8, 16)):
        bm[:, g, :] = ((tp <= tt) & (tp > tt - w)).astype(np.float32)
        bm[:, 4 + g, :] = ((tp - P) > (tt - w)).astype(np.float32)
    c["bmat"] = bm.reshape(P, 8 * P).astype(bf)
    invc = np.zeros((P, 8), np.float32)
    for g, w in enumerate((2, 4, 8, 16)):
        invc[:, g] = 1.0 / w
        if rank == 0:
            invc[:, 4 + g] = 1.0 / np.minimum(np.arange(P) + 1, w)
        else:
            invc[:, 4 + g] = 1.0 / w
    c["invc"] = invc
    sel = np.zeros((P, 3), np.float32)
    if rank >= 1:
        sel[:, rank - 1] = 1.0
    c["sel"] = sel
    coef = np.zeros((64, 12), np.float32)
    for s in range(3):
        if s < rank:
            coef[:, s * 4:(s + 1) * 4] = np.exp(lg * 2048.0 * (rank - 1 - s))[None, :]
    c["coef"] = coef
    return c


_NC_CACHE = {}


def kernel(x, positions, norm1_g, w_in, b_gate, p_ret, p_att, p_pool, pool_lin, pool_scale,
           w_o, norm2_g, w_up, conv_w, conv_b, w_down, final_norm_g):
    bf = ml_dtypes.bfloat16
    f32 = np.float32
    x = np.asarray(x, f32)
    positions = np.asarray(positions, np.int32)
    if "nc" not in _NC_CACHE:
        _NC_CACHE["nc"] = build_program()
    nc = _NC_CACHE["nc"]

    def tile128(v):
        v = np.asarray(v, f32)
        return np.ascontiguousarray(np.broadcast_to(v[:, None, :], (v.shape[0], P, v.shape[1])))

    shared = {
        "w_in": np.ascontiguousarray(np.asarray(w_in, f32)),
        "p_ret": np.ascontiguousarray(np.asarray(p_ret, f32)),
        "p_att": np.ascontiguousarray(np.asarray(p_att, f32)),
        "p_pool": np.ascontiguousarray(np.asarray(p_pool, f32)),
        "w_o": np.ascontiguousarray(np.asarray(w_o, f32)),
        "w_up": np.ascontiguousarray(np.asarray(w_up, f32)),
        "w_down": np.ascontiguousarray(np.asarray(w_down, f32)),
        "g1b": tile128(norm1_g),
        "g2b": tile128(norm2_g),
        "gfb": np.ascontiguousarray(np.broadcast_to(np.asarray(final_norm_g, f32)[None, :], (P, D))),
        "bgate": np.ascontiguousarray(np.asarray(b_gate, f32).reshape(DEPTH, 24, P).transpose(0, 2, 1)),
        "pscale": np.ascontiguousarray(np.asarray(pool_scale, f32).reshape(DEPTH, 2, P).transpose(0, 2, 1)),
        "convw": np.ascontiguousarray(np.asarray(conv_w, f32).reshape(DEPTH, 3, 44, P).transpose(0, 3, 1, 2).reshape(DEPTH, P, 132)),
        "convb": np.ascontiguousarray(np.asarray(conv_b, f32).reshape(DEPTH, 44, P).transpose(0, 2, 1)),
    }
    pl = np.asarray(pool_lin, f32)
    plb = np.zeros((DEPTH, P, 2, P), f32)
    for bl in range(2):
        for gg in range(2):
            plb[:, gg * 64:(gg + 1) * 64, bl, gg * 64:(gg + 1) * 64] = pl[:, bl * 2 + gg]
    shared["plblk"] = plb.reshape(DEPTH, P, 256).astype(bf)

    in_maps = []
    for c in range(NCORES):
        b, rank = divmod(c, 4)
        t0 = rank * T
        m = dict(shared)
        m["x"] = np.ascontiguousarray(x[b, t0:t0 + T, :])
        m["pos"] = np.ascontiguousarray(positions[b, t0:t0 + T].reshape(NT, P).T)
        m.update(_host_consts(rank))
        in_maps.append(m)
    res = run_bass_kernel_spmd(nc, in_maps, core_ids=list(range(NCORES)))
    out = np.zeros((2, 4 * T, D), f32)
    for c in range(NCORES):
        b, rank = divmod(c, 4)
        out[b, rank * T:(rank + 1) * T, :] = res.results[c]["out"]
    return out
```

```python
import math
import numpy as np
import ml_dtypes
import concourse.bass as bass
import concourse.mybir as mybir
from concourse.bass_utils import run_bass_kernel_spmd

F32 = mybir.dt.float32
BF = mybir.dt.bfloat16
I32 = mybir.dt.int32
ALU = mybir.AluOpType
AF = mybir.ActivationFunctionType
AX = mybir.AxisListType

P = 128
T = 2048
NT = 16
D = 1024
DIN = 7168
DFF = 2816
DEPTH = 2
EPS = 1e-6
NCORES = 8
GD = (1, 4, 16)
KVW = 516
ENG = ['pe', 'act', 'dve', 'pool', 'sp']
import os
STOP = os.environ.get('KSTOP', '')


class _Stop(Exception):
    pass


def mark(tag):
    if STOP == tag:
        raise _Stop()

R = 6


class Sched:
    def __init__(self):
        self.ops = {e: [] for e in ENG}
        self.lastw = {}
        self.readers = {}
        self.dmas = {e: [] for e in ENG}
        self.ncc = 0

    def op(self, eng, fn, r=(), w=(), kind='c'):
        deps = set()
        for b in r:
            if b in self.lastw:
                deps.add(self.lastw[b])
        for b in w:
            if b in self.lastw:
                deps.add(self.lastw[b])
            for x in self.readers.get(b, ()):
                deps.add(x)
        me = (eng, len(self.ops[eng]))
        if eng == 'pe':
            deps = {d for d in deps if d[0] != 'pe'}
        rec = dict(fn=fn, deps=deps, kind=kind, sig=False)
        if kind == 'd':
            j = len(self.dmas[eng])
            rec['dj'] = j
            if j >= R:
                deps.add(self.dmas[eng][j - R])
            self.dmas[eng].append(me)
            rec['sig'] = True
        elif kind == 'cc':
            rec['cj'] = self.ncc
            self.ncc += 1
            rec['sig'] = True
        deps.discard(me)
        for d in deps:
            self.ops[d[0]][d[1]]['sig'] = True
        self.ops[eng].append(rec)
        for b in r:
            self.readers.setdefault(b, []).append(me)
        for b in w:
            self.lastw[b] = me
            self.readers[b] = []
        return me

    def finalize(self):
        for e in ENG:
            c = 0
            for rec in self.ops[e]:
                if rec['kind'] == 'c':
                    if rec['sig']:
                        c += 1
                        rec['sv'] = (('c', e), c, 1)
                elif rec['kind'] == 'd':
                    j = rec['dj']
                    rec['sv'] = (('d', e, j % R), 16 * (j // R + 1), 16)
                else:
                    rec['sv'] = (('cc', rec['cj']), 1, 1)

    def emit(self, e, eng, sems):
        waited = {}
        for rec in self.ops[e]:
            need = {}
            for d in rec['deps']:
                sid, val, _ = self.ops[d[0]][d[1]]['sv']
                need[sid] = max(need.get(sid, 0), val)
            for sid, val in need.items():
                if waited.get(sid, 0) < val:
                    eng.wait_ge(sems[sid], val)
                    waited[sid] = val
            inst = rec['fn'](eng)
            if rec['sig']:
                sid, val, inc = rec['sv']
                inst.then_inc(sems[sid], inc)


def build_program():
    nc = bass.Bass("TRN2", target_bir_lowering=False)
    S = Sched()

    def din(name, shape, dt=F32):
        return nc.dram_tensor(name, list(shape), dt, kind="ExternalInput").ap()

    x_d = din("x", [T, D])
    pos_d = din("pos", [P, NT], I32)
    w_in_d = din("w_in", [DEPTH, D, DIN])
    p_ret_d = din("p_ret", [DEPTH, 512, D])
    p_att_d = din("p_att", [DEPTH, 256, D])
    p_pool_d = din("p_pool", [DEPTH, 256, D])
    w_o_d = din("w_o", [DEPTH, D, D])
    w_up_d = din("w_up", [DEPTH, D, 2 * DFF])
    w_down_d = din("w_down", [DEPTH, DFF, D])
    g1_d = din("g1b", [DEPTH, P, D])
    g2_d = din("g2b", [DEPTH, P, D])
    gf_d = din("gfb", [P, D])
    bgate_d = din("bgate", [DEPTH, P, 24])
    pscale_d = din("pscale", [DEPTH, P, 2])
    convw_d = din("convw", [DEPTH, P, 3 * 44])
    convb_d = din("convb", [DEPTH, P, 44])
    plblk_d = din("plblk", [DEPTH, P, 256], BF)
    ident_d = din("ident", [P, P], BF)
    mask2_d = din("mask2", [P, 256], BF)
    invf_d = din("invf", [P, 32])
    dec8_d = din("dec8", [P, 8])
    gtab_d = din("gtab", [64, 512])
    bmat_d = din("bmat", [P, 8 * P], BF)
    invc_d = din("invc", [P, 8])
    sel_d = din("sel", [P, 3])
    coef_d = din("coef", [64, 12])
    out_d = nc.dram_tensor("out", [T, D], F32, kind="ExternalOutput").ap()

    Qd, KVd, HB, HBall, Od, SX, SXall, CX, CXall, RVd, hTd = [], [], [], [], [], [], [], [], [], [], []
    for l in range(DEPTH):
        Qd.append(nc.dram_tensor(f"Qd{l}", [T, 768], BF).ap())
        KVd.append(nc.dram_tensor(f"KVd{l}", [T, 3 * KVW], BF).ap())
        HB.append(nc.dram_tensor(f"HB{l}", [2688, KVW], BF).ap())
        HBall.append([nc.dram_tensor(f"HBall{l}_{c}", [4 * (512 if c < 5 else 128), KVW], BF).ap() for c in range(6)])
        Od.append(nc.dram_tensor(f"Od{l}", [T, 780], F32).ap())
        SX.append(nc.dram_tensor(f"SX{l}", [P, 768], F32).ap())
        SXall.append(nc.dram_tensor(f"SXall{l}", [4 * P, 768], F32).ap())
        CX.append(nc.dram_tensor(f"CX{l}", [P, 16], BF).ap())
        CXall.append(nc.dram_tensor(f"CXall{l}", [4 * P, 16], BF).ap())
        RVd.append(nc.dram_tensor(f"RVd{l}", [T, 1024], BF).ap())
        hTd.append(nc.dram_tensor(f"hTd{l}", [P, 8 * T], BF).ap().rearrange("p (k n) -> p k n", k=8))

    cur = [16512]

    def sb(name, shape, dt, at=None):
        nbytes = int(np.prod(shape[1:])) * (4 if dt in (F32, I32) else 2)
        nbytes = (nbytes + 31) // 32 * 32
        if at is None:
            off = cur[0]
            cur[0] += nbytes
        else:
            off = at
        assert off + nbytes <= 229344, (name, off, nbytes)
        return nc.alloc_sbuf_tensor_at(name, list(shape), dt, offset=off), off + nbytes

    def sbt(name, shape, dt):
        return sb(name, shape, dt)[0]

    x_sb = sbt("x_sb", [P, NT, D], F32)
    ident = sbt("ident", [P, P], BF)
    mask2 = sbt("mask2", [P, 256], BF)
    selI = sbt("selI", [P, 3, P], BF)
    cos_t = sbt("cos_t", [P, NT, 32], F32)
    sin_t = sbt("sin_t", [P, NT, 32], F32)
    ang_off = cur[0]
    ang_t = sbt("ang_t", [P, NT, 32], F32)
    invf = sbt("invf", [P, 32], F32)
    pos_i = sbt("pos_i", [P, NT], I32)
    pos_f = sbt("pos_f", [P, NT], F32)
    gb1 = sbt("gb1", [P, D], F32)
    gb2 = sbt("gb2", [P, D], F32)
    bgate = sbt("bgate", [P, 24], F32)
    pscale = sbt("pscale", [P, 2], F32)
    convw = sbt("convw", [P, 132], F32)
    convb = sbt("convb", [P, 44], F32)
    plblk = sbt("plblk", [P, 2, P], BF)
    bmat = sbt("bmat", [P, 8, P], BF)
    invc = sbt("invc", [P, 8], F32)
    dec8 = sbt("dec8", [P, 8], F32)
    sel = sbt("sel", [P, 3], F32)
    coef = sbt("coef", [64, 12], F32)
    gtab = sbt("gtab", [64, 512], F32)
    ss = sbt("ss", [P, 1], F32)
    rs = sbt("rs", [P, 1], F32)
    ss1 = sbt("ss1", [P, 1], F32)
    rs1 = sbt("rs1", [P, 1], F32)
    junk = nc.alloc_sbuf_tensor_at("junk", [P, D], BF, offset=ang_off)
    h_bf = sbt("h_bf", [P, D], BF)
    Sst = sbt("Sst", [64, 512], F32)
    Sbf = sbt("Sbf", [64, 512], BF)
    halo_p = sbt("halo_p", [P, 16], BF)
    cxs = sbt("cxs", [P, 16], BF)
    cxc = [sbt(f"cxc{i}", [P, 16], BF) for i in range(3)]
    cxf = sbt("cxf", [P, 16], F32)
    baseAB = cur[0]
    qT = sbt("qT", [64, 4, T], BF)
    kT = sbt("kT", [64, 4, T], BF)
    wb = [sbt(f"wb{i}", [P, 8, 512], BF) for i in range(2)]
    base2 = cur[0]
    hT_all = sbt("hT_all", [P, 8, T], BF)
    ktok = sbt("ktok", [P, NT, 256], BF)
    rvt = [sbt(f"rvt{i}", [P, 1024], BF) for i in range(4)]
    vst = [sbt(f"vst{i}", [P, 2, 4, 65], BF) for i in range(2)]
    puf = sbt("puf", [P, 256], F32)
    rA = sbt("rA", [P, 512], F32)
    rB = sbt("rB", [P, 512], F32)
    rC = sbt("rC", [P, 512], F32)
    obf = [sbt(f"obf{i}", [P, 512], BF) for i in range(4)]
    zf = [sbt(f"zf{i}", [P, 512], F32) for i in range(2)]
    zt = sbt("zt", [P, 512], F32)
    endA = cur[0]
    cur[0] = base2
    qblk = [sbt(f"qblk{i}", [P, 256], BF) for i in range(3)]
    kvo = [sbt(f"kvo{i}", [P, KVW], BF) for i in range(6)]
    kvc = [[sbt(f"kvc{j}_{i}", [P, KVW], BF) for i in range(3)] for j in range(2)]
    vprev = [sbt(f"vprev{i}", [P, 260], BF) for i in range(3)]
    QT = [sbt(f"QT{i}", [64, 4, P], BF) for i in range(2)]
    KTo = [sbt(f"KTo{i}", [64, 4, P], BF) for i in range(6)]
    KTh = [sbt(f"KTh{i}", [64, 4, P], BF) for i in range(2)]
    Pe = [sbt(f"Pe{i}", [P, 1024], BF) for i in range(2)]
    Pm = [sbt(f"Pm{i}", [P, 1024], BF) for i in range(2)]
    osb = [sbt(f"osb{i}", [P, 260], F32) for i in range(2)]
    endT = cur[0]
    cur[0] = base2
    puh = sbt("puh", [P, 256], BF)
    hTg = sbt("hTg", [P, 8, 512], BF)
    yretT = sbt("yretT", [P, 4, 512], BF)
    yattT = sbt("yattT", [P, 2, 512], BF)
    ypoolT = sbt("ypoolT", [P, 2, 512], BF)
    mT_off = cur[0]
    mT = sbt("mT", [P, 8, 512], BF)
    sxc = nc.alloc_sbuf_tensor_at("sxc", [P, 768], F32, offset=mT_off)
    rvk = [sbt(f"rvk{i}", [P, 768], BF) for i in range(2)]
    pus = [sbt(f"pus{i}", [P, 256], BF) for i in range(4)]
    sg = [sbt(f"sg{i}", [P, 512], F32) for i in range(2)]
    PT = [sbt(f"PT{i}", [P, 512], BF) for i in range(2)]
    ycp = [sbt(f"ycp{i}", [P, 512], F32) for i in range(2)]
    ysq = [sbt(f"ysq{i}", [P, 512], F32) for i in range(2)]
    st4 = [sbt(f"st4{i}", [P, 24], F32) for i in range(2)]
    yrb = [sbt(f"yrb{i}", [P, 512], BF) for i in range(2)]
    Ot = [sbt(f"Ot{i}", [P, 780], F32) for i in range(2)]
    num = [sbt(f"num{i}", [P, 260], F32) for i in range(2)]
    rden = [sbt(f"rden{i}", [P, 4], F32) for i in range(2)]
    yab = [sbt(f"yab{i}", [P, 256], BF) for i in range(2)]
    ppb = [sbt(f"ppb{i}", [P, 256], BF) for i in range(2)]
    ppT = [sbt(f"ppT{i}", [P, 2, P], BF) for i in range(2)]
    wg = sbt("wg", [P, 8, 3, P], BF)
    wp = sbt("wp", [P, 8, P], BF)
    sigt = sbt("sigt", [P, 3, 512], BF)
    sig = [sigt[:, i, :] for i in range(3)]
    signm = [("sig", i) for i in range(3)]
    m0, m1 = ycp[0], ysq[0]
    endB = cur[0]
    cur[0] = baseAB
    wd = sbt("wd", [P, 22, D], BF)
    h2T = sbt("h2T", [P, 8, 514], BF)
    gT = sbt("gT", [P, 22, 512], BF)
    wua = [sbt(f"wua{i}", [P, 8, 256], BF) for i in range(2)]
    wub = [sbt(f"wub{i}", [P, 8, 256], BF) for i in range(2)]
    uba = sbt("uba", [P, 514], F32)
    ubb = sbt("ubb", [P, 514], F32)
    ca = sbt("ca", [P, 512], F32)
    cb = sbt("cb", [P, 512], F32)
    sa = sbt("sa", [P, 512], F32)
    yout = [sbt(f"yout{i}", [P, D], F32) for i in range(2)]
    endC = cur[0]
    print("SBUF map: baseAB", baseAB, "base2", base2, "endA", endA, "endT", endT, "endB", endB, "endC", endC)

    ps = [nc.alloc_psum_tensor(f"ps{i}", [P, 512], F32) for i in range(4)]
    pS2 = nc.alloc_psum_tensor("pS2", [P, 1024], F32)
    pt = [nc.alloc_psum_tensor(f"pt{i}", [P, 1024], BF) for i in range(2)]

    cnt = {'ps': 0, 'pt': 0, 'wb': 0, 'ob': 0, 'zf': 0}

    def nxt(k, n):
        v = cnt[k] % n
        cnt[k] += 1
        return v

    op = S.op

    allbufs = set()

    def ALLBUF():
        return list(allbufs)

    def reg(*names):
        for n in names:
            allbufs.add(n)

    def dma(eng, out, in_, r, w):
        reg(*r)
        reg(*w)
        return op(eng, lambda e_: e_.dma_start(out=out, in_=in_), r=r, w=w, kind='d')

    def cop(eng, fn, r, w):
        reg(*r)
        reg(*w)
        return op(eng, fn, r=r, w=w)

    def load_const(dst, src, name):
        dma('sp', dst, src, [], [name])

    def fullbar(skip=()):
        def bufs():
            return [n for n in ALLBUF() if not (isinstance(n, tuple) and n and n[0] in skip)]
        cop('act', lambda e_: e_.activation(out=ss[:, 0:1], in_=ident[:, 0:1], func=AF.Copy), bufs(), bufs())
        cop('dve', lambda e_: e_.memset(ss[:, :], 0.0), bufs(), bufs())
        cop('pool', lambda e_: e_.memset(rs[:, :], 0.0), bufs(), bufs())
        cop('pe', lambda e_: e_.transpose(out=pt[0][:, 0:P], in_=ident[:, :], identity=ident[:, :]), bufs(), bufs())
        dma('sp', ss[:, :], ss[:, :], bufs(), bufs())

    load_const(ident[:, :], ident_d, "ident")
    load_const(mask2[:, :], mask2_d, "mask2")
    load_const(invf[:, :], invf_d, "invf")
    load_const(pos_i[:, :], pos_d, "pos_i")
    load_const(bmat[:, :, :], bmat_d.rearrange("p (g n) -> p g n", g=8), "bmat")
    load_const(invc[:, :], invc_d, "invc")
    load_const(dec8[:, :], dec8_d, "dec8")
    load_const(sel[:, :], sel_d, "sel")
    load_const(coef[:, :], coef_d, "coef")
    load_const(gtab[:, :], gtab_d, "gtab")
    for t in range(NT):
        dma('sp', x_sb[:, t, :], x_d[t * P:(t + 1) * P, :], [], [("x", t)])
    for s in range(3):
        cop('dve', lambda e_, s=s: e_.tensor_scalar(out=selI[:, s, :], in0=ident[:, :], scalar1=sel[:, s:s + 1],
                                                     scalar2=None, op0=ALU.mult),
            ["ident", "sel"], [("selI", s)])
    cop('dve', lambda e_: e_.tensor_copy(out=pos_f[:, :], in_=pos_i[:, :]), ["pos_i"], ["pos_f"])
    for t in range(NT):
        cop('dve', lambda e_, t=t: e_.tensor_scalar(out=ang_t[:, t, :], in0=invf[:, :], scalar1=pos_f[:, t:t + 1],
                                                     scalar2=None, op0=ALU.mult),
            ["invf", "pos_f"], [("ang", t)])
    angs = [("ang", t) for t in range(NT)]
    C1 = 6.28125
    C2 = 2 * math.pi - C1
    rA2 = rA[:, :].rearrange("p (t f) -> p t f", t=NT)
    rBi = rB[:, :].bitcast(I32).rearrange("p (t f) -> p t f", t=NT)
    for dst, shift, nm in ((sin_t, 0.0, "sin"), (cos_t, 0.5 * math.pi, "cos")):
        cop('dve', lambda e_, dst=dst, shift=shift: e_.tensor_scalar(out=dst[:, :, :], in0=ang_t[:, :, :], scalar1=shift, scalar2=None,
                                                                     op0=ALU.add), angs, [nm + "_a"])
        cop('dve', lambda e_, dst=dst: e_.tensor_scalar(out=rA2, in0=dst[:, :, :], scalar1=1.0 / (2 * math.pi), scalar2=None, op0=ALU.mult),
            [nm + "_a"], ["trA"])
        cop('dve', lambda e_: e_.tensor_copy(out=rBi, in_=rA2), ["trA"], ["trB"])
        cop('dve', lambda e_: e_.tensor_copy(out=rA2, in_=rBi), ["trB"], ["trA"])
        cop('dve', lambda e_, dst=dst: e_.scalar_tensor_tensor(out=dst[:, :, :], in0=rA2, scalar=-C1, in1=dst[:, :, :], op0=ALU.mult, op1=ALU.add),
            ["trA", nm + "_a"], [nm + "_a"])
        cop('dve', lambda e_, dst=dst: e_.scalar_tensor_tensor(out=dst[:, :, :], in0=rA2, scalar=-C2, in1=dst[:, :, :], op0=ALU.mult, op1=ALU.add),
            ["trA", nm + "_a"], [nm + "_a"])
        cop('dve', lambda e_, dst=dst: e_.tensor_scalar(out=dst[:, :, :], in0=dst[:, :, :], scalar1=-math.pi, scalar2=math.pi, op0=ALU.max, op1=ALU.min),
            [nm + "_a"], [nm + "_a"])
        cop('act', lambda e_, dst=dst: e_.activation(out=dst[:, :, :], in_=dst[:, :, :], func=AF.Sin), [nm + "_a"], [nm])

    def norm_tile(l, t, gbuf, gname, k=0):
        hb = h_bf if k == 0 else junk
        ss_, rs_ = (ss, rs) if k == 0 else (ss1, rs1)
        hn_, sn_, rn_ = ("h_bf", "ss", "rs") if k == 0 else (("h_bf", 1), ("ss", 1), ("rs", 1))
        cop('dve', lambda e_: e_.memset(ss_[:, :], 0.0), [], [sn_])
        cop('act', lambda e_: e_.activation(out=hb[:, :], in_=x_sb[:, t, :], func=AF.Square, accum_out=ss_[:, 0:1]),
            [("x", t), sn_], [sn_, hn_])
        cop('act', lambda e_: e_.activation(out=rs_[:, :], in_=ss_[:, :], func=AF.Sqrt, bias=EPS, scale=1.0 / D), [sn_], [rn_])
        cop('dve', lambda e_: e_.reciprocal(out=rs_[:, :], in_=rs_[:, :]), [rn_], [rn_])
        cop('dve', lambda e_: e_.scalar_tensor_tensor(out=hb[:, :], in0=x_sb[:, t, :], scalar=rs_[:, 0:1],
                                                      in1=gbuf[:, :], op0=ALU.mult, op1=ALU.mult),
            [rn_, ("x", t), gname], [hn_])

    def final_tile(t):
        yi = t % 2
        ss_, rs_ = (ss, rs) if yi == 0 else (ss1, rs1)
        sn_, rn_ = ("ss", "rs") if yi == 0 else (("ss", 1), ("rs", 1))
        cop('dve', lambda e_: e_.memset(ss_[:, :], 0.0), [], [sn_])
        cop('act', lambda e_: e_.activation(out=yout[yi][:, :], in_=x_sb[:, t, :], func=AF.Square, accum_out=ss_[:, 0:1]),
            [("x", t), sn_], [sn_, ("yout", yi)])
        cop('act', lambda e_: e_.activation(out=rs_[:, :], in_=ss_[:, :], func=AF.Sqrt, bias=EPS, scale=1.0 / D), [sn_], [rn_])
        cop('dve', lambda e_: e_.reciprocal(out=rs_[:, :], in_=rs_[:, :]), [rn_], [rn_])
        cop('dve', lambda e_: e_.scalar_tensor_tensor(out=yout[yi][:, :], in0=x_sb[:, t, :], scalar=rs_[:, 0:1], in1=gb1[:, :],
                                                      op0=ALU.mult, op1=ALU.mult), [rn_, ("x", t), "gb1"], [("yout", yi)])
        dma('sp', out_d[t * P:(t + 1) * P, :], yout[yi][:, :], [("yout", yi)], [("out", t)])

    def transpose_to(src_ap_fn, nblk, src_names, dst_ap, dst_names, rows=P, width=P, evac='act'):
        pi = nxt('pt', 2)
        ptt = pt[pi]

        def f(e_):
            ins = None
            for i in range(nblk):
                ins = e_.transpose(out=ptt[0:width, i * P:(i + 1) * P], in_=src_ap_fn(i), identity=ident[:, :])
            return ins
        cop('pe', f, list(src_names) + ["ident"], [("pt", pi)])
        src_v = ptt[0:width, 0:nblk * P].rearrange("p (k n) -> p k n", k=nblk) if len(dst_ap.shape) == 3 else ptt[0:width, 0:nblk * P]
        if evac == 'act':
            cop('act', lambda e_: e_.activation(out=dst_ap, in_=src_v, func=AF.Copy), [("pt", pi)], list(dst_names))
        else:
            cop('dve', lambda e_: e_.tensor_copy(out=dst_ap, in_=src_v), [("pt", pi)], list(dst_names))

    def load_w(src_ap, dst_ap, name):
        dma('pool', dst_ap, src_ap, [], [name])

    def mm_group(out_ap, pairs, r, w):
        n = len(pairs)

        def f(e_):
            ins = None
            for i, (a, b) in enumerate(pairs):
                ins = e_.matmul(out_ap, a, b, start=(i == 0), stop=(i == n - 1))
            return ins
        cop('pe', f, r, w)

    def rotary(psrc, H, t, dst_ap, r, w, scale_ap=None):
        zi = nxt('zf', 2)
        z3 = zf[zi][:, 0:H * 64].rearrange("p (h d) -> p h d", h=H)
        cop('act', lambda e_: e_.activation(out=zf[zi][:, 0:H * 64], in_=psrc, func=AF.Copy), list(r), [("zf", zi)])
        A3 = rA[:, 0:H * 64].rearrange("p (h d) -> p h d", h=H)
        B3 = rB[:, 0:H * 64].rearrange("p (h d) -> p h d", h=H)
        C3 = rC[:, 0:H * 64].rearrange("p (h d) -> p h d", h=H)
        cb_ = cos_t[:, t, :].unsqueeze(1).to_broadcast([P, H, 32])
        sb_ = sin_t[:, t, :].unsqueeze(1).to_broadcast([P, H, 32])
        d3 = dst_ap.rearrange("p (h d) -> p h d", h=H)
        lo, hi = slice(0, 32), slice(32, 64)
        for eng, me, other, opc, tag in (('dve', lo, hi, ALU.subtract, "0"), ('pool', hi, lo, ALU.add, "1")):
            cop(eng, lambda e_, me=me: e_.tensor_tensor(out=A3[:, :, me], in0=z3[:, :, me], in1=cb_, op=ALU.mult), [("zf", zi), "cos"], ["rA" + tag])
            cop(eng, lambda e_, me=me, other=other: e_.tensor_tensor(out=B3[:, :, me], in0=z3[:, :, other], in1=sb_, op=ALU.mult), [("zf", zi), "sin"], ["rB" + tag])
            if scale_ap is None:
                cop(eng, lambda e_, me=me, opc=opc: e_.tensor_tensor(out=d3[:, :, me], in0=A3[:, :, me], in1=B3[:, :, me], op=opc),
                    ["rA" + tag, "rB" + tag], [(w[0], tag)])
            else:
                sc = scale_ap.unsqueeze(2).to_broadcast([P, H, 32])
                cop(eng, lambda e_, me=me, opc=opc: e_.tensor_tensor(out=C3[:, :, me], in0=A3[:, :, me], in1=B3[:, :, me], op=opc),
                    ["rA" + tag, "rB" + tag], ["rC" + tag])
                cop(eng, lambda e_, me=me, sc=sc: e_.tensor_tensor(out=d3[:, :, me], in0=C3[:, :, me], in1=sc, op=ALU.mult),
                    ["rC" + tag, "dec8"], [(w[0], tag)])

    def win_chunk(l, c0, width=512):
        return w_in_d[l, :, c0:c0 + width].rearrange("(k p) n -> p k n", p=P)

    def layer(l):
        L = f"L{l}"
        dma('sp', gb1[:, :], g1_d[l], [], ["gb1"])
        dma('sp', gb2[:, :], g2_d[l], [], ["gb2"])
        dma('sp', bgate[:, :], bgate_d[l], [], ["bgate"])
        dma('sp', pscale[:, :], pscale_d[l], [], ["pscale"])
        dma('sp', convw[:, :], convw_d[l], [], ["convw"])
        dma('sp', convb[:, :], convb_d[l], [], ["convb"])
        dma('sp', plblk[:, :, :], plblk_d[l].rearrange("p (b n) -> p b n", b=2), [], ["plblk"])

        cop('act', lambda e_: e_.activation(out=ss[:, 0:1], in_=ident[:, 0:1], func=AF.Copy), ALLBUF(), ALLBUF())
        cop('dve', lambda e_: e_.memset(Sst[:, :], 0.0), ALLBUF(), ALLBUF())
        cop('pool', lambda e_: e_.memset(rs[:, :], 0.0), ALLBUF(), ALLBUF())
        for i in range(2):
            cop('dve', lambda e_, i=i: e_.memset(vst[i][:, :, :, :], 1.0), [], [("vst", i)])
        cop('dve', lambda e_: e_.memset(zt[:, :], 0.0), [], ["zt"])
        dma('sp', SX[l][64:128, 0:512], zt[64:128, :], ["zt"], [("SX", L, 2)])

        mark(f'A0_{l}')
        def hb_copies():
            kvnames = [("KVd", L, t, c) for t in range(NT) for c in (0, KVW, 2 * KVW, 256, KVW + 256, 2 * KVW + 256)]
            KV16 = KVd[l].rearrange("(m d) c -> d m c", d=16)
            KV4 = KVd[l].rearrange("(m d) c -> d m c", d=4)
            for r_ in range(16):
                dma('sp', HB[l][r_ * P:(r_ + 1) * P, :], KV16[r_, :, 2 * KVW:3 * KVW], kvnames, [("HB", L, r_)])
            for r_ in range(4):
                dma('sp', HB[l][(16 + r_) * P:(17 + r_) * P, :], KV4[r_, 384:512, KVW:2 * KVW], kvnames, [("HB", L, 16 + r_)])
            dma('sp', HB[l][20 * P:21 * P, :], KVd[l][T - P:T, 0:KVW], kvnames, [("HB", L, 20)])

        def kv_collectives():
            for c in range(6):
                nr = 512 if c < 5 else 128
                reg(("HBall", L, c))
                op('pool', lambda e_, l=l, c=c, nr=nr: e_.collective_compute(
                    "AllGather", ALU.bypass, replica_groups=[[0, 1, 2, 3], [4, 5, 6, 7]],
                    ins=[HB[l][c * 512:c * 512 + nr, :]], outs=[HBall[l][c][:, :]], dma_qos="P3"),
                   r=[("HB", L, bi) for bi in range(c * 4, min(c * 4 + 4, 21))], w=[("HBall", L, c)], kind='cc')

        chunks = [(3072, 'av01'), (3584, 'av2pu'), (2048, 'aq2k0'), (2560, 'ak12'), (0, 'rqrk'), (1536, 'aq01'), (512, 'rv')]
        deferred = []

        def flush_deferred():
            while deferred:
                deferred.pop(0)()

        def proc_tile(c0, kind, wi, t):
            pi = nxt('ps', 4)
            mm_group(ps[pi][:, :], [(hT_all[:, k, t * P:(t + 1) * P], wb[wi][:, k, :]) for k in range(8)],
                     [("hT", t), ("wb", wi)], [("ps", pi)])
            flush_deferred()
            rows = slice(t * P, (t + 1) * P)
            if kind == 'rqrk':
                oi = nxt('ob', 4)
                rotary(ps[pi][:, :], 8, t, obf[oi][:, :], [("ps", pi)], [("obf", oi)], scale_ap=dec8[:, :])
                cop('act', lambda e_, oi=oi, t=t: e_.activation(out=ktok[:, t, :], in_=obf[oi][:, 256:512], func=AF.Copy),
                    [(("obf", oi), "0"), (("obf", oi), "1")], [("ktok", t)])

                def later(oi=oi, t=t):
                    transpose_to(lambda i: obf[oi][:, i * 64:(i + 1) * 64], 4, [(("obf", oi), "0"), (("obf", oi), "1")],
                                 qT[:, :, t * P:(t + 1) * P], [("qT", t)], width=64)
                    transpose_to(lambda i: obf[oi][:, 256 + i * 64:256 + (i + 1) * 64], 4, [(("obf", oi), "0"), (("obf", oi), "1")],
                                 kT[:, :, t * P:(t + 1) * P], [("kT", t)], width=64, evac='dve')
                deferred.append(later)
            elif kind == 'rv':
                ri = t % 4
                cop('act', lambda e_, pi=pi, ri=ri: e_.activation(out=rvt[ri][:, 0:512], in_=ps[pi][:, :], func=AF.Copy),
                    [("ps", pi)], [("rvt", ri)])
                cop('act', lambda e_, ri=ri, t=t: e_.activation(out=rvt[ri][:, 512:768], in_=ktok[:, t, :], func=AF.Copy),
                    [("ktok", t)], [("rvt", ri)])
                def later(ri=ri, t=t):
                    def fS(e_):
                        ins = None
                        for h in range(4):
                            ins = e_.matmul(pS2[0:64, h * P:(h + 1) * P], ktok[:, t, h * 64:(h + 1) * 64],
                                            rvt[ri][:, h * P:(h + 1) * P], start=True, stop=True)
                        return ins
                    cop('pe', fS, [("rvt", ri), ("ktok", t)], ["pS2"])
                    cop('dve', lambda e_: e_.tensor_tensor(out=Sst[:, :], in0=Sst[:, :], in1=pS2[0:64, 0:512], op=ALU.add),
                        ["pS2", "Sst"], ["Sst"])
                    cop('dve', lambda e_: e_.tensor_tensor(out=Sst[:, :], in0=Sst[:, :], in1=gtab[:, :], op=ALU.mult),
                        ["Sst", "gtab"], ["Sst"])
                deferred.append(later)
            elif kind in ('aq01', 'aq2k0', 'ak12'):
                oi = nxt('ob', 4)
                rotary(ps[pi][:, :], 8, t, obf[oi][:, :], [("ps", pi)], [("obf", oi)])
                dsts = {'aq01': [(Qd[l], 0, "Qd"), (Qd[l], 256, "Qd")],
                        'aq2k0': [(Qd[l], 512, "Qd"), (KVd[l], 0, "KVd")],
                        'ak12': [(KVd[l], KVW, "KVd"), (KVd[l], 2 * KVW, "KVd")]}[kind]
                for hh, (dt_, c, nm) in enumerate(dsts):
                    dma('sp', dt_[rows, c:c + 256], obf[oi][:, hh * 256:(hh + 1) * 256],
                        [(("obf", oi), "0"), (("obf", oi), "1")], [(nm, L, t, c)])
            elif kind == 'av01':
                vi = nxt('ob', 2)
                cop('act', lambda e_, pi=pi, vi=vi: e_.activation(
                    out=vst[vi][:, :, :, 0:64], in_=ps[pi][:, :].rearrange("p (g h d) -> p g h d", g=2, h=4), func=AF.Copy),
                    [("ps", pi)], [("vst", vi)])
                for g in range(2):
                    dma('sp', KVd[l][rows, g * KVW + 256:(g + 1) * KVW], vst[vi][:, g, :, :].rearrange("p h d -> p (h d)"),
                        [("vst", vi)], [("KVd", L, t, g * KVW + 256)])
            else:
                vi = nxt('ob', 2)
                cop('act', lambda e_, pi=pi, vi=vi: e_.activation(
                    out=vst[vi][:, 0, :, 0:64], in_=ps[pi][:, 0:256].rearrange("p (h d) -> p h d", h=4), func=AF.Copy),
                    [("ps", pi)], [("vst", vi)])
                dma('sp', KVd[l][rows, 2 * KVW + 256:3 * KVW], vst[vi][:, 0, :, :].rearrange("p h d -> p (h d)"),
                    [("vst", vi)], [("KVd", L, t, 2 * KVW + 256)])
                oi = nxt('ob', 4)
                cop('act', lambda e_, pi=pi, oi=oi: e_.activation(out=obf[oi][:, 0:256], in_=ps[pi][:, 256:512], func=AF.Copy),
                    [("ps", pi)], [(("obf", oi), "0"), (("obf", oi), "1")])
                dma('sp', RVd[l][rows, 768:1024], obf[oi][:, 0:256], [(("obf", oi), "0"), (("obf", oi), "1")], [("RVd", L, t, 768)])
                if t == NT - 1:
                    cop('dve', lambda e_, pi=pi: e_.tensor_copy(out=puf[:, :], in_=ps[pi][:, 256:512]), [("ps", pi)], ["puf"])
                    dma('sp', SX[l][:, 512:768], puf[:, :], ["puf"], [("SX", L, 1)])
            if kind == 'rv':
                dma('sp', RVd[l][rows, 0:768], rvt[t % 4][:, 0:768], [("rvt", t % 4)], [("RVd", L, t, 0)])

        wis = [nxt('wb', 2) for _ in chunks]
        load_w(win_chunk(l, chunks[0][0]), wb[wis[0]][:, :, :], ("wb", wis[0]))
        for ci_, (c0, kind) in enumerate(chunks):
            wi = wis[ci_]
            if ci_ + 1 < len(chunks):
                load_w(win_chunk(l, chunks[ci_ + 1][0]), wb[wis[ci_ + 1]][:, :, :], ("wb", wis[ci_ + 1]))
            if kind == 'rqrk':
                hb_copies()
            if kind == 'aq01':
                kv_collectives()
            def a0_transp(t):
                hbt = h_bf if t % 2 == 0 else junk
                hbn = "h_bf" if t % 2 == 0 else ("h_bf", 1)
                transpose_to(lambda i, hbt=hbt: hbt[:, i * P:(i + 1) * P], 8, [hbn], hT_all[:, :, t * P:(t + 1) * P], [("hT", t)])
                if t % 4 == 3:
                    g4 = t // 4
                    dma('sp', hTd[l][:, :, g4 * 512:(g4 + 1) * 512], hT_all[:, :, g4 * 512:(g4 + 1) * 512],
                        [("hT", tq) for tq in range(g4 * 4, g4 * 4 + 4)], [("hTd", L, g4)])
            if ci_ == 0:
                norm_tile(l, 0, gb1, "gb1", k=0)
                norm_tile(l, 1, gb1, "gb1", k=1)
                a0_transp(0)
            for t in range(NT):
                if ci_ == 0:
                    if t + 2 < NT:
                        pass
                    if t + 1 < NT:
                        a0_transp(t + 1)
                    if t + 2 < NT:
                        norm_tile(l, t + 2, gb1, "gb1", k=t % 2)
                proc_tile(c0, kind, wi, t)
            flush_deferred()
            if kind == 'rv':
                dma('sp', SX[l][0:64, 0:512], Sst[:, :], ["Sst"], [("SX", L, 0)])

        mark(f'A1_{l}')
        reg(("SXall", L))
        op('pool', lambda e_, l=l: e_.collective_compute("AllGather", ALU.bypass, replica_groups=[[0, 1, 2, 3], [4, 5, 6, 7]],
                                                          ins=[SX[l][:, :]], outs=[SXall[l][:, :]]),
           r=[("SX", L, 0), ("SX", L, 1), ("SX", L, 2)], w=[("SXall", L)], kind='cc')

        fullbar(skip=("HBall", "SXall", "HB", "SX"))

        mark(f'X1_{l}')
        blocks = []
        for g in range(2):
            nb_ = 16 // GD[g]
            for r_ in range(GD[g]):
                blocks.append((g, r_, 0, 'warm'))
                for b in range(1, nb_):
                    blocks.append((g, r_, b, 'blk'))
                blocks.append((g, r_, 0, 'blk'))
        for r_ in range(16):
            blocks.append((2, r_, 0, 'blk'))

        def att_stage0(i):
            g, r_, b, mode = blocks[i]
            d = GD[g]
            ci, k3 = i % 3, i % 6
            KVv = KVd[l].rearrange("(n d) c -> d n c", d=d)
            tiles = list(range(b * d, (b + 1) * d))
            ms = slice(b * P, (b + 1) * P)
            kn = [("KVd", L, t, g * KVW) for t in tiles] + [("KVd", L, t, g * KVW + 256) for t in tiles]
            if mode != 'warm':
                Qv = Qd[l].rearrange("(n d) c -> d n c", d=d)
                qn = [("Qd", L, t, g * 256) for t in tiles]
                dma('sp', qblk[ci][:, :], Qv[r_, ms, g * 256:(g + 1) * 256], qn, [("qblk", ci)])
            dma('sp', kvo[k3][:, :], KVv[r_, ms, g * KVW:(g + 1) * KVW], kn, [("kvo", k3)])
            if mode != 'warm' and b == 0:
                hs_ = i % 2
                bi = {0: 20, 1: 16 + r_, 2: r_}[g]
                hc, hoff = bi // 4, (bi % 4) * P
                hnr = 512 if hc < 5 else 128
                for s_ in range(3):
                    dma('sp', kvc[hs_][s_][:, :], HBall[l][hc][s_ * hnr + hoff:s_ * hnr + hoff + P, :], [("HBall", L, hc)], [("kvc", hs_, s_)])

        def att_stage1a(i):
            g, r_, b, mode = blocks[i]
            d = GD[g]
            ci, k3, qi = i % 2, i % 6, i % 3
            if mode == 'warm':
                KVv = KVd[l].rearrange("(n d) c -> d n c", d=d)
                tiles = list(range(b * d, (b + 1) * d))
                ms = slice(b * P, (b + 1) * P)
                kn = [("KVd", L, t, g * KVW) for t in tiles] + [("KVd", L, t, g * KVW + 256) for t in tiles]
                transpose_to(lambda j: kvo[k3][:, j * 64:(j + 1) * 64], 4, [("kvo", k3)], KTo[k3][:, :, :], [("KTo", k3)], width=64, evac='dve')
                return
            Qv = Qd[l].rearrange("(n d) c -> d n c", d=d)
            KVv = KVd[l].rearrange("(n d) c -> d n c", d=d)
            tiles = list(range(b * d, (b + 1) * d))
            ms = slice(b * P, (b + 1) * P)
            qn = [("Qd", L, t, g * 256) for t in tiles]
            kn = [("KVd", L, t, g * KVW) for t in tiles] + [("KVd", L, t, g * KVW + 256) for t in tiles]
            transpose_to(lambda j: qblk[qi][:, j * 64:(j + 1) * 64], 4, [("qblk", qi)], QT[ci][:, :, :], [("QT", ci)], width=64)
            transpose_to(lambda j: kvo[k3][:, j * 64:(j + 1) * 64], 4, [("kvo", k3)], KTo[k3][:, :, :], [("KTo", k3)], width=64, evac='dve')
            sA, sB = (ps[0], ps[1]) if ci == 0 else (ps[2], ps[3])
            sAn, sBn = (("ps", 0), ("ps", 1)) if ci == 0 else (("ps", 2), ("ps", 3))
            pO = pS2[:, ci * 512:(ci + 1) * 512]
            pOn = ("pS2", ci)
            if b >= 1:
                pass
            else:
                hi_ = i % 2
                v3 = i % 3
                bi = {0: 20, 1: 16 + r_, 2: r_}[g]
                hc, hoff = bi // 4, (bi % 4) * P
                hnr = 512 if hc < 5 else 128
                hs_ = i % 2

                def fK(e_):
                    ins = None
                    for h in range(4):
                        for s_ in range(3):
                            ins = e_.matmul(pO[0:64, h * P:(h + 1) * P], kvc[hs_][s_][:, h * 64:(h + 1) * 64], selI[:, s_, :],
                                            start=(s_ == 0), stop=(s_ == 2))
                    return ins
                cop('pe', fK, [("kvc", hs_, 0), ("kvc", hs_, 1), ("kvc", hs_, 2), ("selI", 0), ("selI", 1), ("selI", 2)], [pOn])
                cop('act', lambda e_: e_.activation(out=KTh[hi_][:, :, :], in_=pO[0:64, :].rearrange("p (h n) -> p h n", h=4), func=AF.Copy),
                    [pOn], [("KTh", hi_)])

                def fV(e_):
                    ins = None
                    for s_ in range(3):
                        ins = e_.matmul(pO[:, 0:260], selI[:, s_, :], kvc[hs_][s_][:, 256:KVW], start=(s_ == 0), stop=(s_ == 2))
                    return ins
                cop('pe', fV, [("kvc", hs_, 0), ("kvc", hs_, 1), ("kvc", hs_, 2), ("selI", 0), ("selI", 1), ("selI", 2)], [pOn])
                cop('act', lambda e_: e_.activation(out=vprev[v3][:, :], in_=pO[:, 0:260], func=AF.Copy), [pOn], [("vprev", v3)])

        def att_stage1b(i):
            g, r_, b, mode = blocks[i]
            if mode == 'warm':
                return
            ci, k3 = i % 2, i % 6
            sA, sB = (ps[0], ps[1]) if ci == 0 else (ps[2], ps[3])
            sAn, sBn = (("ps", 0), ("ps", 1)) if ci == 0 else (("ps", 2), ("ps", 3))
            if b >= 1:
                p3 = (i - 1) % 6
                KTp, KTpn = KTo[p3], ("KTo", p3)
            else:
                KTp, KTpn = KTh[i % 2], ("KTh", i % 2)

            def fS(e_):
                ins = None
                for h in range(4):
                    dst = sA if h < 2 else sB
                    o = (h % 2) * 256
                    e_.matmul(dst[:, o:o + P], KTp[:, h, :], QT[ci][:, h, :], start=True, stop=True)
                    ins = e_.matmul(dst[:, o + P:o + 256], KTo[k3][:, h, :], QT[ci][:, h, :], start=True, stop=True)
                return ins
            cop('pe', fS, [KTpn, ("KTo", k3), ("QT", ci)], [sAn, sBn])
            cop('act', lambda e_: e_.activation(out=Pe[ci][:, 0:512], in_=sA[:, :], func=AF.Exp, scale=0.125), [sAn], [("Pe", ci, 0)])
            cop('act', lambda e_: e_.activation(out=Pe[ci][:, 512:1024], in_=sB[:, :], func=AF.Exp, scale=0.125), [sBn], [("Pe", ci, 1)])
            cop('dve', lambda e_: e_.tensor_tensor(out=Pm[ci][:, :].rearrange("p (h n) -> p h n", h=4),
                                                   in0=Pe[ci][:, :].rearrange("p (h n) -> p h n", h=4),
                                                   in1=mask2[:, :].unsqueeze(1).to_broadcast([P, 4, 256]), op=ALU.mult),
                [("Pe", ci, 0), ("Pe", ci, 1), "mask2"], [("Pm", ci)])

        def att_stage2(i):
            g, r_, b, mode = blocks[i]
            if mode == 'warm':
                return
            d = GD[g]
            ci, k3 = i % 2, i % 6
            Ov = Od[l].rearrange("(n d) c -> d n c", d=d)
            tiles = list(range(b * d, (b + 1) * d))
            ms = slice(b * P, (b + 1) * P)
            pO = pS2[:, ci * 512:(ci + 1) * 512]
            pOn = ("pS2", ci)
            if b >= 1:
                p3 = (i - 1) % 6
                vp, vpn = kvo[p3][:, 256:KVW], ("kvo", p3)
            else:
                vp, vpn = vprev[i % 3][:, :], ("vprev", i % 3)

            def fO(e_):
                ins = None
                for h in range(4):
                    e_.matmul(pO[:, h * 65:(h + 1) * 65], Pm[ci][:, h * 256:h * 256 + P], vp[:, h * 65:(h + 1) * 65], start=True, stop=False)
                    ins = e_.matmul(pO[:, h * 65:(h + 1) * 65], Pm[ci][:, h * 256 + P:(h + 1) * 256],
                                    kvo[k3][:, 256 + h * 65:256 + (h + 1) * 65], start=False, stop=True)
                return ins
            cop('pe', fO, [("Pm", ci), vpn, ("kvo", k3)], [pOn])
            cop('act', lambda e_: e_.activation(out=osb[ci][:, :], in_=pO[:, 0:260], func=AF.Copy), [pOn], [("osb", ci)])
            dma('pool', Ov[r_, ms, g * 260:(g + 1) * 260], osb[ci][:, :], [("osb", ci)], [("Od", L, g, t) for t in tiles])

        nblk = len(blocks)
        att_stage0(0)
        for i in range(nblk + 2):
            if i + 1 < nblk:
                att_stage0(i + 1)
            if i < nblk:
                att_stage1a(i)
            if 1 <= i <= nblk:
                att_stage1b(i - 1)
            if 2 <= i <= nblk + 1:
                att_stage2(i - 2)

        fullbar()
        pacc = ycp[0][:, 0:256]
        for s in range(3):
            dma('sp', sxc[:, :], SXall[l][s * P:(s + 1) * P, :], [("SXall", L)], ["sxc"])
            for h in range(4):
                hs = slice(h * P, (h + 1) * P)
                if s == 0:
                    cop('dve', lambda e_, hs=hs, h=h: e_.tensor_scalar(out=Sst[:, hs], in0=sxc[0:64, hs], scalar1=coef[:, h:h + 1],
                                                                       scalar2=None, op0=ALU.mult), ["sxc", "coef"], ["Sst"])
                else:
                    cop('dve', lambda e_, hs=hs, h=h, s=s: e_.scalar_tensor_tensor(
                        out=Sst[:, hs], in0=sxc[0:64, hs], scalar=coef[:, s * 4 + h:s * 4 + h + 1], in1=Sst[:, hs],
                        op0=ALU.mult, op1=ALU.add), ["sxc", "coef", "Sst"], ["Sst"])
            if s == 0:
                cop('dve', lambda e_: e_.tensor_scalar(out=pacc, in0=sxc[:, 512:768], scalar1=sel[:, 0:1], scalar2=None,
                                                       op0=ALU.mult), ["sxc", "sel"], [("ycp", 0)])
            elif s == 1:
                cop('dve', lambda e_: e_.scalar_tensor_tensor(out=pacc, in0=sxc[:, 512:768], scalar=sel[:, 1:2], in1=pacc,
                                                              op0=ALU.mult, op1=ALU.add), ["sxc", "sel", ("ycp", 0)], [("ycp", 0)])
            else:
                cop('dve', lambda e_: e_.scalar_tensor_tensor(out=puh[:, :], in0=sxc[:, 512:768], scalar=sel[:, 2:3], in1=pacc,
                                                              op0=ALU.mult, op1=ALU.add), ["sxc", "sel", ("ycp", 0)], ["puh"])
        cop('act', lambda e_: e_.activation(out=Sbf[:, :], in_=Sst[:, :], func=AF.Copy), ["Sst"], ["Sbf"])

        mark(f'ATT_{l}')
        for gi in range(4):
            wi = nxt('wb', 2)
            load_w(win_chunk(l, 1024), wb[wi][:, :, :], ("wb", wi))
            dma('sp', hTg[:, :, :], hTd[l][:, :, gi * 512:(gi + 1) * 512], [("hTd", L, gi)], [("hTg", tq) for tq in range(4)])

            def do_tile(tt, gi=gi, wi=wi):
                t = gi * 4 + tt
                tc = slice(tt * P, (tt + 1) * P)
                tg = slice(t * P, (t + 1) * P)
                bi_ = t % 2
                sg_, PT_, ycp_, ysq_, st4_, yrb_, Ot_, num_, rden_, yab_, ppb_, ppT_ = (sg[bi_], PT[bi_], ycp[bi_], ysq[bi_], st4[bi_], yrb[bi_],
                                                                                         Ot[bi_], num[bi_], rden[bi_], yab[bi_], ppb[bi_], ppT[bi_])
                ri = t % 2
                dma('sp', rvk[ri][:, 0:768], RVd[l][tg, 0:768], [("RVd", L, t, 0)], [("rvk", ri)])
                dma('sp', pus[t % 4][:, :], RVd[l][tg, 768:1024], [("RVd", L, t, 768)], [("pus", t % 4)])
                dma('sp', Ot_[:, :], Od[l][tg, :], [("Od", L, g, t) for g in range(3)], [("Ot", bi_)])
                yield
                pi = nxt('ps', 4)
                mm_group(ps[pi][:, :], [(hTg[:, k, tc], wb[wi][:, k, :]) for k in range(8)], [("hTg", tt), ("wb", wi)], [("ps", pi)])
                cop('act', lambda e_, pi=pi: e_.activation(out=sg_[:, :], in_=ps[pi][:, :], func=AF.Silu), [("ps", pi)], [("sg", bi_)])
                yield
                if t == 0: mark(f'B1_{l}')
                pi = nxt('ps', 4)

                def fR(e_, pi=pi, tg=tg):
                    ins = None
                    for h in range(4):
                        ins = e_.matmul(ps[pi][:, h * P:(h + 1) * P], kT[:, h, tg], qT[:, h, tg], start=True, stop=True)
                    return ins
                cop('pe', fR, [("kT", t), ("qT", t)], [("ps", pi)])
                cop('dve', lambda e_, pi=pi: e_.tensor_tensor(out=PT_[:, :].rearrange("p (h n) -> p h n", h=4),
                                                              in0=ps[pi][:, :].rearrange("p (h n) -> p h n", h=4),
                                                              in1=mask2[:, P:256].unsqueeze(1).to_broadcast([P, 4, P]), op=ALU.mult),
                    [("ps", pi), "mask2"], [("PT", bi_)])
                yield
                pi = nxt('ps', 4)

                def fY(e_, pi=pi, tg=tg, ri=ri):
                    ins = None
                    for h in range(4):
                        e_.matmul(ps[pi][:, h * P:(h + 1) * P], PT_[:, h * P:(h + 1) * P], rvk[ri][:, h * P:(h + 1) * P], start=True, stop=False)
                        ins = e_.matmul(ps[pi][:, h * P:(h + 1) * P], qT[:, h, tg], Sbf[:, h * P:(h + 1) * P], start=False, stop=True)
                    return ins
                cop('pe', fY, [("PT", bi_), ("rvk", ri), ("qT", t), "Sbf"], [("ps", pi)])
                cop('act', lambda e_, pi=pi: e_.activation(out=ycp_[:, :], in_=ps[pi][:, :], func=AF.Copy), [("ps", pi)], [("ycp", bi_)])
                yield
                if t == 0: mark(f'B2_{l}')
                def fU(e_, ri=ri):
                    ins = None
                    for h in range(4):
                        ins = e_.matmul(pS2[0:64, h * P:(h + 1) * P], rvk[ri][:, 512 + h * 64:512 + (h + 1) * 64],
                                        rvk[ri][:, h * P:(h + 1) * P], start=True, stop=True)
                    return ins
                cop('pe', fU, [("rvk", ri)], ["pS2"])
                cop('dve', lambda e_: e_.tensor_tensor(out=Sst[:, :], in0=Sst[:, :], in1=pS2[0:64, 0:512], op=ALU.add), ["pS2", "Sst"], ["Sst"])
                cop('dve', lambda e_: e_.tensor_tensor(out=Sst[:, :], in0=Sst[:, :], in1=gtab[:, :], op=ALU.mult), ["Sst", "gtab"], ["Sst"])
                cop('act', lambda e_: e_.activation(out=Sbf[:, :], in_=Sst[:, :], func=AF.Copy), ["Sst"], ["Sbf"])
                yield
                if t == 0: mark(f'B3_{l}')
                y3 = ycp_[:, :].rearrange("p (h n) -> p h n", h=4)
                cop('dve', lambda e_: e_.tensor_reduce(out=st4_[:, 0:4], in_=y3, axis=AX.X, op=ALU.add), [("ycp", bi_)], [("s1", bi_)])
                cop('act', lambda e_: e_.activation(out=ysq_[:, :], in_=ycp_[:, :], func=AF.Square), [("ycp", bi_)], [("ysq", bi_)])
                cop('dve', lambda e_: e_.tensor_reduce(out=st4_[:, 4:8], in_=ysq_[:, :].rearrange("p (h n) -> p h n", h=4), axis=AX.X, op=ALU.add),
                    [("ysq", bi_)], [("s2", bi_)])
                cop('dve', lambda e_: e_.tensor_scalar(out=st4_[:, 8:12], in0=st4_[:, 0:4], scalar1=1.0 / P, scalar2=None, op0=ALU.mult), [("s1", bi_)], [("mean", bi_)])
                cop('dve', lambda e_: e_.tensor_tensor(out=st4_[:, 12:16], in0=st4_[:, 8:12], in1=st4_[:, 8:12], op=ALU.mult), [("mean", bi_)], [("msq", bi_)])
                cop('dve', lambda e_: e_.scalar_tensor_tensor(out=st4_[:, 16:20], in0=st4_[:, 4:8], scalar=1.0 / P, in1=st4_[:, 12:16],
                                                              op0=ALU.mult, op1=ALU.subtract), [("s2", bi_), ("msq", bi_)], [("var", bi_)])
                yield
                cop('act', lambda e_: e_.activation(out=st4_[:, 16:20], in_=st4_[:, 16:20], func=AF.Sqrt, bias=EPS, scale=1.0), [("var", bi_)], [("rstd", bi_)])
                cop('dve', lambda e_: e_.reciprocal(out=st4_[:, 16:20], in_=st4_[:, 16:20]), [("rstd", bi_)], [("rstd", bi_)])
                cop('dve', lambda e_: e_.scalar_tensor_tensor(out=st4_[:, 20:24], in0=st4_[:, 8:12], scalar=-1.0, in1=st4_[:, 16:20],
                                                              op0=ALU.mult, op1=ALU.mult), [("mean", bi_), ("rstd", bi_)], [("nmr", bi_)])
                yield
                yq3 = ysq_[:, :].rearrange("p (h n) -> p h n", h=4)
                cop('dve', lambda e_: e_.tensor_tensor(out=yq3, in0=y3, in1=st4_[:, 16:20].unsqueeze(2).to_broadcast([P, 4, P]), op=ALU.mult),
                    [("ycp", bi_), ("rstd", bi_), ("s2", bi_)], [("ysq", bi_)])
                cop('dve', lambda e_: e_.tensor_tensor(out=yq3, in0=yq3, in1=st4_[:, 20:24].unsqueeze(2).to_broadcast([P, 4, P]), op=ALU.add),
                    [("ysq", bi_), ("nmr", bi_)], [("ysq", bi_)])
                cop('dve', lambda e_: e_.tensor_tensor(out=yrb_[:, :], in0=ysq_[:, :], in1=sg_[:, :], op=ALU.mult),
                    [("ysq", bi_), ("sg", bi_)], [("yrb", bi_), ("ysq", bi_)])
                yield
                transpose_to(lambda i: yrb_[:, i * P:(i + 1) * P], 4, [("yrb", bi_)], yretT[:, :, tc], [("yretT", tt)])
                yield
                if t == 0: mark(f'B4_{l}')
                O3 = Ot_[:, :].rearrange("p (g c) -> p g c", g=3)
                cop('dve', lambda e_: e_.tensor_tensor(out=num_[:, :], in0=O3[:, 0, :], in1=O3[:, 1, :], op=ALU.add), [("Ot", bi_)], [("num", bi_)])
                cop('dve', lambda e_: e_.tensor_tensor(out=num_[:, :], in0=num_[:, :], in1=O3[:, 2, :], op=ALU.add), [("Ot", bi_), ("num", bi_)], [("num", bi_)])
                n3 = num_[:, :].rearrange("p (h c) -> p h c", h=4)
                cop('dve', lambda e_: e_.reciprocal(out=rden_[:, :], in_=n3[:, :, 64]), [("num", bi_)], [("rden", bi_)])
                cop('dve', lambda e_: e_.tensor_tensor(out=yab_[:, :].rearrange("p (h c) -> p h c", h=4), in0=n3[:, :, 0:64],
                                                       in1=rden_[:, :].unsqueeze(2).to_broadcast([P, 4, 64]), op=ALU.mult),
                    [("num", bi_), ("rden", bi_)], [("yab", bi_)])
                yield
                transpose_to(lambda i: yab_[:, i * P:(i + 1) * P], 2, [("yab", bi_)], yattT[:, :, tc], [("yattT", tt)])
                yield
                if t == 0: mark(f'B5_{l}')
                pu_t = pus[t % 4][:, :]
                if t == 0:
                    pu_p, pun = puh[:, :], "puh"
                else:
                    pu_p, pun = pus[(t - 1) % 4][:, :], ("pus", (t - 1) % 4)
                pi = nxt('ps', 4)

                def fP(e_, pi=pi, pu_t=pu_t, pu_p=pu_p):
                    ins = None
                    for g in range(4):
                        e_.matmul(ps[pi][:, g * 64:(g + 1) * 64], bmat[:, g, :], pu_t[:, g * 64:(g + 1) * 64], start=True, stop=False)
                        ins = e_.matmul(ps[pi][:, g * 64:(g + 1) * 64], bmat[:, 4 + g, :], pu_p[:, g * 64:(g + 1) * 64], start=False, stop=True)
                    return ins
                cop('pe', fP, [("pus", t % 4), pun, "bmat"], [("ps", pi)])
                ic = 4 if t == 0 else 0
                pp3 = ppb_[:, :].rearrange("p (g c) -> p g c", g=4)
                cop('dve', lambda e_, pi=pi, ic=ic: e_.tensor_tensor(out=pp3, in0=ps[pi][:, 0:256].rearrange("p (g c) -> p g c", g=4),
                                                                   in1=invc[:, ic:ic + 4].unsqueeze(2).to_broadcast([P, 4, 64]), op=ALU.mult),
                    [("ps", pi), "invc"], [("ppb", bi_)])
                cop('dve', lambda e_, pu_t=pu_t: e_.tensor_tensor(out=ppb_[:, :], in0=ppb_[:, :], in1=pu_t, op=ALU.subtract),
                    [("ppb", bi_), ("pus", t % 4)], [("ppb", bi_)])
                yield
                transpose_to(lambda i: ppb_[:, i * P:(i + 1) * P], 2, [("ppb", bi_)], ppT_[:, :, :], [("ppT", bi_)])
                pi = nxt('ps', 4)

                def fL(e_, pi=pi):
                    ins = None
                    for bl in range(2):
                        ins = e_.matmul(ps[pi][:, bl * P:(bl + 1) * P], plblk[:, bl, :], ppT_[:, bl, :], start=True, stop=True)
                    return ins
                cop('pe', fL, ["plblk", ("ppT", bi_)], [("ps", pi)])
                for bl in range(2):
                    cop('act', lambda e_, pi=pi, bl=bl, tc=tc: e_.activation(out=ypoolT[:, bl, tc], in_=ps[pi][:, bl * P:(bl + 1) * P],
                                                                            func=AF.Copy, scale=pscale[:, bl:bl + 1]),
                        [("ps", pi), "pscale"], [("ypoolT", tt, bl)])
            pend_tiles = [0, 1, 2, 3]
            active = []

            def step(g_):
                try:
                    next(g_)
                    return True
                except StopIteration:
                    return False
            g0 = do_tile(pend_tiles.pop(0))
            step(g0)
            step(g0)
            active.append(g0)
            active.append(do_tile(pend_tiles.pop(0)))
            while active:
                for g_ in list(active):
                    if not step(g_):
                        active.remove(g_)
                        if pend_tiles:
                            active.append(do_tile(pend_tiles.pop(0)))
            if gi == 0: mark(f'B6_{l}')
            hTn = [("hTg", tt) for tt in range(4)]
            for j in range(8):
                js = slice(j * P, (j + 1) * P)
                for i in range(3):
                    load_w(w_in_d[l, :, 4096 + i * 1024 + j * P:4096 + i * 1024 + (j + 1) * P].rearrange("(k p) n -> p k n", p=P),
                           wg[:, :, i, :], ("wg", i))
                load_w(p_ret_d[l, :, js].rearrange("(k p) n -> p k n", p=P), wp[:, 0:4, :], "wp0")
                load_w(p_att_d[l, :, js].rearrange("(k p) n -> p k n", p=P), wp[:, 4:6, :], "wp1")
                load_w(p_pool_d[l, :, js].rearrange("(k p) n -> p k n", p=P), wp[:, 6:8, :], "wp2")
                pg = []
                for i in range(3):
                    pi = nxt('ps', 4)
                    mm_group(ps[pi][:, :], [(wg[:, k, i, :], hTg[:, k, :]) for k in range(8)], hTn + [("wg", i)], [("ps", pi)])
                    cop('act', lambda e_, pi=pi, i=i, j=j: e_.activation(out=sig[i], in_=ps[pi][:, :], func=AF.Sigmoid,
                                                                        bias=bgate[:, i * 8 + j:i * 8 + j + 1]),
                        [("ps", pi), "bgate"], [signm[i]])
                brs = [(yretT, 0, 4, [("yretT", tt) for tt in range(4)], "wp0"),
                       (yattT, 4, 2, [("yattT", tt) for tt in range(4)], "wp1"),
                       (ypoolT, 6, 2, [("ypoolT", tt, bl) for tt in range(4) for bl in range(2)], "wp2")]
                for i, (yT, k0, nk, names, wn) in enumerate(brs):
                    pi = nxt('ps', 4)
                    mm_group(ps[pi][:, :], [(wp[:, k0 + k, :], yT[:, k, :]) for k in range(nk)], names + [wn], [("ps", pi)])
                    dst = m0 if i == 0 else m1
                    dn = ("ycp", 0) if i == 0 else ("ysq", 0)
                    cop('dve', lambda e_, pi=pi, i=i, dst=dst: e_.tensor_tensor(out=dst[:, :], in0=ps[pi][:, :], in1=sig[i], op=ALU.mult),
                        [("ps", pi), signm[i]], [dn])
                    if i == 1:
                        cop('dve', lambda e_: e_.tensor_tensor(out=m0[:, :], in0=m0[:, :], in1=m1[:, :], op=ALU.add), [("ycp", 0), ("ysq", 0)], [("ycp", 0)])
                    if i == 2:
                        cop('dve', lambda e_, j=j: e_.tensor_tensor(out=mT[:, j, :], in0=m0[:, :], in1=m1[:, :], op=ALU.add), [("ycp", 0), ("ysq", 0)], [("mT", j)])
            if gi == 0: mark(f'B7_{l}')
            mTn = [("mT", j) for j in range(8)]
            for half in range(2):
                wi2 = nxt('wb', 2)
                load_w(w_o_d[l, :, half * 512:(half + 1) * 512].rearrange("(k p) n -> p k n", p=P), wb[wi2][:, :, :], ("wb", wi2))
                for tt in range(4):
                    t = gi * 4 + tt
                    tc = slice(tt * P, (tt + 1) * P)
                    pi = nxt('ps', 4)
                    mm_group(ps[pi][:, :], [(mT[:, k, tc], wb[wi2][:, k, :]) for k in range(8)], mTn + [("wb", wi2)], [("ps", pi)])
                    cop('dve', lambda e_, pi=pi, t=t, half=half: e_.tensor_tensor(out=x_sb[:, t, half * 512:(half + 1) * 512],
                                                                                  in0=x_sb[:, t, half * 512:(half + 1) * 512],
                                                                                  in1=ps[pi][:, :], op=ALU.add),
                        [("ps", pi), ("x", t)], [("x", t)])

        mark(f'B_{l}')
        norm_tile(l, NT - 1, gb2, "gb2")
        transpose_to(lambda i: h_bf[:, i * P:(i + 1) * P], 8, ["h_bf"], hTg[:, :, 0:P], [("hTg", 0)])
        cop('act', lambda e_: e_.activation(out=cxs[:, :].rearrange("p (k n) -> p k n", k=8), in_=hTg[:, :, P - 2:P], func=AF.Copy),
            [("hTg", 0)], ["cxs"])
        dma('sp', CX[l][:, :], cxs[:, :], ["cxs"], [("CX", L)])
        reg(("CXall", L))
        op('pool', lambda e_, l=l: e_.collective_compute("AllGather", ALU.bypass, replica_groups=[[0, 1, 2, 3], [4, 5, 6, 7]],
                                                          ins=[CX[l][:, :]], outs=[CXall[l][:, :]]),
           r=[("CX", L)], w=[("CXall", L)], kind='cc')
        fullbar(skip=("CX", "CXall"))

        def halo_select():
            for s_ in range(3):
                dma('sp', cxc[s_][:, :], CXall[l][s_ * P:(s_ + 1) * P, :], [("CXall", L)], [("cxc", s_)])
            cop('dve', lambda e_: e_.tensor_scalar(out=cxf[:, :], in0=cxc[0][:, :], scalar1=sel[:, 0:1], scalar2=None, op0=ALU.mult),
                [("cxc", 0), "sel"], ["cxf"])
            for s_ in (1, 2):
                cop('dve', lambda e_, s_=s_: e_.scalar_tensor_tensor(out=cxf[:, :], in0=cxc[s_][:, :], scalar=sel[:, s_:s_ + 1], in1=cxf[:, :],
                                                                     op0=ALU.mult, op1=ALU.add), [("cxc", s_), "sel", "cxf"], ["cxf"])
            cop('dve', lambda e_: e_.tensor_copy(out=h2T[:, :, 0:2], in_=cxf[:, :].rearrange("p (k n) -> p k n", k=8)), ["cxf"], ["h2halo"])

        mark(f'X2_{l}')
        if l == DEPTH - 1:
            dma('sp', gb1[:, :], gf_d, [], ["gb1"])
        wdn = [("wd", q) for q in range(11)]
        def c_norm_tile(gi, tt):
            norm_tile(l, gi * 4 + tt, gb2, "gb2", k=tt % 2)

        def c_transp(gi, tt):
            hbt = h_bf if tt % 2 == 0 else junk
            hbn = "h_bf" if tt % 2 == 0 else ("h_bf", 1)
            transpose_to(lambda i: hbt[:, i * P:(i + 1) * P], 8, [hbn], h2T[:, :, 2 + tt * P:2 + (tt + 1) * P], [("h2T", tt)])

        corder = [1, 2, 3, 0]
        norm_tile(l, 3, gb2, "gb2", k=1)
        c_transp(0, 3)
        cop('dve', lambda e_: e_.tensor_copy(out=h2T[:, :, 0:2], in_=h2T[:, :, 512:514]), [("h2T", 3)], ["h2halo"])
        c_norm_tile(corder[0], 0)
        for tt in range(4):
            if tt < 3:
                c_norm_tile(corder[0], tt + 1)
            c_transp(corder[0], tt)
        for cidx, gi in enumerate(corder):
            gnext = corder[cidx + 1] if cidx + 1 < 4 else None
            h2n = [("h2T", tt) for tt in range(4)] + ["h2halo"]
            for q in range(11):
                ui = q % 2
                load_w(w_up_d[l, :, q * 256:(q + 1) * 256].rearrange("(k p) n -> p k n", p=P), wua[ui][:, :, :], ("wua", ui))
                load_w(w_up_d[l, :, DFF + q * 256:DFF + (q + 1) * 256].rearrange("(k p) n -> p k n", p=P), wub[ui][:, :, :], ("wub", ui))
                if cidx == 0:
                    load_w(w_down_d[l, q * 256:(q + 1) * 256, :].rearrange("(k p) n -> p k n", p=P), wd[:, 2 * q:2 * q + 2, :], ("wd", q))
                for jj in range(2):
                    j = 2 * q + jj
                    cs = slice(jj * P, (jj + 1) * P)
                    pa = nxt('ps', 4)
                    mm_group(ps[pa][:, :], [(wua[ui][:, k, cs], h2T[:, k, 2:514]) for k in range(8)], h2n + [("wua", ui)], [("ps", pa)])
                    pb = nxt('ps', 4)
                    mm_group(ps[pb][:, :], [(wub[ui][:, k, cs], h2T[:, k, 2:514]) for k in range(8)], h2n + [("wub", ui)], [("ps", pb)])
                    mm_group(pS2[:, 0:2], [(wua[ui][:, k, cs], h2T[:, k, 0:2]) for k in range(8)], h2n + [("wua", ui)], ["pS2a"])
                    mm_group(pS2[:, 512:514], [(wub[ui][:, k, cs], h2T[:, k, 0:2]) for k in range(8)], h2n + [("wub", ui)], ["pS2b"])
                    for (ub, pp_, hoff, hn, nm, jc, cc) in ((uba, pa, 0, "pS2a", "uba", j, ca), (ubb, pb, 512, "pS2b", "ubb", 22 + j, cb)):
                        cop('act', lambda e_, ub=ub, pp_=pp_: e_.activation(out=ub[:, 2:514], in_=ps[pp_][:, :], func=AF.Copy),
                            [("ps", pp_)], [nm + "m"])
                        cop('act', lambda e_, ub=ub, hoff=hoff: e_.activation(out=ub[:, 0:2], in_=pS2[:, hoff:hoff + 2], func=AF.Copy),
                            [hn], [nm + "h"])
                        cop('act', lambda e_, ub=ub, jc=jc, cc=cc: e_.activation(out=cc[:, :], in_=ub[:, 2:514], func=AF.Identity,
                                                                                bias=convb[:, jc:jc + 1], scale=convw[:, 88 + jc:89 + jc]),
                            [nm + "m", "convw", "convb"], [nm + "c"])
                        cop('dve', lambda e_, ub=ub, jc=jc, cc=cc: e_.scalar_tensor_tensor(out=cc[:, :], in0=ub[:, 1:513], scalar=convw[:, 44 + jc:45 + jc],
                                                                                          in1=cc[:, :], op0=ALU.mult, op1=ALU.add),
                            [nm + "m", nm + "h", nm + "c", "convw"], [nm + "c"])
                        cop('dve', lambda e_, ub=ub, jc=jc, cc=cc: e_.scalar_tensor_tensor(out=cc[:, :], in0=ub[:, 0:512], scalar=convw[:, jc:jc + 1],
                                                                                          in1=cc[:, :], op0=ALU.mult, op1=ALU.add),
                            [nm + "m", nm + "h", nm + "c", "convw"], [nm + "c"])
                    cop('act', lambda e_: e_.activation(out=sa[:, :], in_=ca[:, :], func=AF.Silu), ["ubac"], ["sa"])
                    cop('dve', lambda e_, j=j: e_.tensor_tensor(out=gT[:, j, :], in0=sa[:, :], in1=cb[:, :], op=ALU.mult),
                        ["sa", "ubbc"], [("gT", j)])
            gTn = [("gT", j) for j in range(22)]
            if gnext is not None:
                if gnext == 0:
                    halo_select()
                else:
                    cop('dve', lambda e_: e_.tensor_copy(out=h2T[:, :, 0:2], in_=h2T[:, :, 512:514]), [("h2T", 3)], ["h2halo"])
                c_norm_tile(gnext, 0)
            for tt in range(4):
                t = gi * 4 + tt
                tc = slice(tt * P, (tt + 1) * P)
                for half in range(2):
                    pi = nxt('ps', 4)
                    mm_group(ps[pi][:, :], [(gT[:, j, tc], wd[:, j, half * 512:(half + 1) * 512]) for j in range(22)], gTn + wdn, [("ps", pi)])
                    cop('dve', lambda e_, pi=pi, t=t, half=half: e_.tensor_tensor(out=x_sb[:, t, half * 512:(half + 1) * 512],
                                                                                  in0=x_sb[:, t, half * 512:(half + 1) * 512],
                                                                                  in1=ps[pi][:, :], op=ALU.add),
                        [("ps", pi), ("x", t)], [("x", t)])
                if gnext is not None:
                    if tt < 3:
                        c_norm_tile(gnext, tt + 1)
                    c_transp(gnext, tt)
                if l == DEPTH - 1:
                    final_tile(t)
        mark(f'C_{l}')
        fullbar()

    try:
        for l in range(DEPTH):
            layer(l)
    except _Stop:
        pass

    cop('act', lambda e_: e_.activation(out=ss[:, 0:1], in_=ident[:, 0:1], func=AF.Copy), ALLBUF(), ALLBUF())

    S.finalize()
    sem_ids = set()
    for e in ENG:
        for rec in S.ops[e]:
            if 'sv' in rec:
                sem_ids.add(rec['sv'][0])
    sems = {}
    ctxs = []
    for sid in sorted(sem_ids, key=str):
        nm = "s_" + "_".join(str(x) for x in sid)
        cm = nc.semaphore(nm)
        sems[sid] = cm.__enter__()
        ctxs.append(cm)
    with nc.Block() as block:
        @block.tensor
        def _(eng):
            S.emit('pe', eng, sems)

        @block.scalar
        def _(eng):
            S.emit('act', eng, sems)

        @block.vector
        def _(eng):
            S.emit('dve', eng, sems)

        @block.gpsimd
        def _(eng):
            S.emit('pool', eng, sems)

        @block.sync
        def _(eng):
            S.emit('sp', eng, sems)
    for cm in reversed(ctxs):
        cm.__exit__(None, None, None)
    return nc


def _host_consts(rank):
    bf = ml_dtypes.bfloat16
    c = {}
    c["ident"] = np.eye(P, dtype=np.float32).astype(bf)
    j = np.arange(P)[:, None]
    i = np.arange(P)[None, :]
    mprev = (j >= i).astype(np.float32)
    mown = (j <= i).astype(np.float32)
    c["mask2"] = np.concatenate([mprev, mown], axis=1).astype(bf)
    inv = (10000.0 ** (-(np.arange(32, dtype=np.float32) / np.float32(32)))).astype(np.float32)
    c["invf"] = np.tile(inv[None, :], (P, 1)).astype(np.float32)
    gam = 1.0 - 2.0 ** (-5.0 - np.arange(4, dtype=np.float64))
    lg = np.log(gam)
    p = np.arange(P, dtype=np.float64)[:, None]
    qdec = np.exp(lg[None, :] * (p + 1.0))
    kdec = np.exp(-lg[None, :] * (p + 1.0)) * (64.0 ** -0.5)
    c["dec8"] = np.concatenate([qdec, kdec], axis=1).astype(np.float32)
    G = np.exp(lg * 128.0)
    c["gtab"] = np.tile(np.repeat(G, P)[None, :], (64, 1)).astype(np.float32)
    bm = np.zeros((P, 8, P), np.float32)
    tp = np.arange(P)[:, None]
    tt = np.arange(P)[None, :]
    for g, w in enumerate((2, 4, 8, 16)):
        bm[:, g, :] = ((tp <= tt) & (tp > tt - w)).astype(np.float32)
        bm[:, 4 + g, :] = ((tp - P) > (tt - w)).astype(np.float32)
    c["bmat"] = bm.reshape(P, 8 * P).astype(bf)
    invc = np.zeros((P, 8), np.float32)
    for g, w in enumerate((2, 4, 8, 16)):
        invc[:, g] = 1.0 / w
        if rank == 0:
            invc[:, 4 + g] = 1.0 / np.minimum(np.arange(P) + 1, w)
        else:
            invc[:, 4 + g] = 1.0 / w
    c["invc"] = invc
    sel = np.zeros((P, 3), np.float32)
    if rank >= 1:
        sel[:, rank - 1] = 1.0
    c["sel"] = sel
    coef = np.zeros((64, 12), np.float32)
    for s in range(3):
        if s < rank:
            coef[:, s * 4:(s + 1) * 4] = np.exp(lg * 2048.0 * (rank - 1 - s))[None, :]
    c["coef"] = coef
    return c


_NC_CACHE = {}


def kernel(x, positions, norm1_g, w_in, b_gate, p_ret, p_att, p_pool, pool_lin, pool_scale,
           w_o, norm2_g, w_up, conv_w, conv_b, w_down, final_norm_g):
    bf = ml_dtypes.bfloat16
    f32 = np.float32
    x = np.asarray(x, f32)
    positions = np.asarray(positions, np.int32)
    if "nc" not in _NC_CACHE:
        _NC_CACHE["nc"] = build_program()
    nc = _NC_CACHE["nc"]

    def tile128(v):
        v = np.asarray(v, f32)
        return np.ascontiguousarray(np.broadcast_to(v[:, None, :], (v.shape[0], P, v.shape[1])))

    shared = {
        "w_in": np.ascontiguousarray(np.asarray(w_in, f32)),
        "p_ret": np.ascontiguousarray(np.asarray(p_ret, f32)),
        "p_att": np.ascontiguousarray(np.asarray(p_att, f32)),
        "p_pool": np.ascontiguousarray(np.asarray(p_pool, f32)),
        "w_o": np.ascontiguousarray(np.asarray(w_o, f32)),
        "w_up": np.ascontiguousarray(np.asarray(w_up, f32)),
        "w_down": np.ascontiguousarray(np.asarray(w_down, f32)),
        "g1b": tile128(norm1_g),
        "g2b": tile128(norm2_g),
        "gfb": np.ascontiguousarray(np.broadcast_to(np.asarray(final_norm_g, f32)[None, :], (P, D))),
        "bgate": np.ascontiguousarray(np.asarray(b_gate, f32).reshape(DEPTH, 24, P).transpose(0, 2, 1)),
        "pscale": np.ascontiguousarray(np.asarray(pool_scale, f32).reshape(DEPTH, 2, P).transpose(0, 2, 1)),
        "convw": np.ascontiguousarray(np.asarray(conv_w, f32).reshape(DEPTH, 3, 44, P).transpose(0, 3, 1, 2).reshape(DEPTH, P, 132)),
        "convb": np.ascontiguousarray(np.asarray(conv_b, f32).reshape(DEPTH, 44, P).transpose(0, 2, 1)),
    }
    pl = np.asarray(pool_lin, f32)
    plb = np.zeros((DEPTH, P, 2, P), f32)
    for bl in range(2):
        for gg in range(2):
            plb[:, gg * 64:(gg + 1) * 64, bl, gg * 64:(gg + 1) * 64] = pl[:, bl * 2 + gg]
    shared["plblk"] = plb.reshape(DEPTH, P, 256).astype(bf)

    in_maps = []
    for c in range(NCORES):
        b, rank = divmod(c, 4)
        t0 = rank * T
        m = dict(shared)
        m["x"] = np.ascontiguousarray(x[b, t0:t0 + T, :])
        m["pos"] = np.ascontiguousarray(positions[b, t0:t0 + T].reshape(NT, P).T)
        m.update(_host_consts(rank))
        in_maps.append(m)
    res = run_bass_kernel_spmd(nc, in_maps, core_ids=list(range(NCORES)))
    out = np.zeros((2, 4 * T, D), f32)
    for c in range(NCORES):
        b, rank = divmod(c, 4)
        out[b, rank * T:(rank + 1) * T, :] = res.results[c]["out"]
    return out
```
